# Optimizing a Trainium2 kernel written in Bass

```python
import jax, jax.numpy as jnp
from jax import lax
import numpy as np

D_MODEL = 1024
BATCH = 32
SEQ = 256
DEPTH = 4
DEC_BATCH = 4
DEC_SEQ = 4096
PAST_LEN = 512

GRID_W = 64
ROPE_BASE = 10000.0
EPS = 1e-6
QBLK = 128

MLA_HEADS = 8
MLA_NOPE = 64
MLA_ROPE = 32
MLA_QK = MLA_NOPE + MLA_ROPE
MLA_V = 64
Q_LORA = 384
KV_LORA = 256
MLA_WIDTH = MLA_HEADS * MLA_V

RET_HEADS = 4
RET_DK = 64
RET_DV = 128
RET_CHUNK = 128
RET_WIDTH = RET_HEADS * RET_DV

WIN_HEADS = 16
WIN_KV_HEADS = 4
WIN_HEAD_DIM = 64
WINDOW = 128
WIN_WIDTH = WIN_HEADS * WIN_HEAD_DIM

AB_SIZES = (Q_LORA, KV_LORA, MLA_ROPE, MLA_WIDTH, RET_HEADS * RET_DK, RET_HEADS * RET_DK, RET_WIDTH, RET_WIDTH)
AB_IN = sum(AB_SIZES)
WIN_SIZES = (WIN_WIDTH, WIN_KV_HEADS * WIN_HEAD_DIM, WIN_KV_HEADS * WIN_HEAD_DIM, WIN_WIDTH)
WIN_IN = sum(WIN_SIZES)
N_EVEN = (DEPTH + 1) // 2
N_ODD = DEPTH // 2

kernel_name = 'hybrid_mla_retention_window_dit_step'

F32 = jnp.float32


def split_cols(y, sizes):
    idx = np.cumsum(sizes)[:-1].tolist()
    return jnp.split(y, idx, axis=-1)


def rms_norm(x, g):
    xf = x.astype(F32)
    y = xf * lax.rsqrt(jnp.mean(xf * xf, axis=-1, keepdims=True) + EPS)
    return (y * g.astype(F32)).astype(x.dtype)


def grid_positions(n_tokens):
    rows = n_tokens // GRID_W
    t = jnp.arange(rows * GRID_W)
    return (t // GRID_W).astype(F32), (t % GRID_W).astype(F32)


def rope_1d(x, pos):
    half = x.shape[-1] // 2
    inv = jnp.power(jnp.float32(ROPE_BASE), -jnp.arange(half, dtype=F32) / half)
    ang = pos[:, None] * inv[None, :]
    cos = jnp.cos(ang)[:, None, :]
    sin = jnp.sin(ang)[:, None, :]
    xf = x.astype(F32)
    x1, x2 = xf[..., :half], xf[..., half:]
    return jnp.concatenate([x1 * cos - x2 * sin, x1 * sin + x2 * cos], axis=-1).astype(x.dtype)


def axial_rope(x, pos):
    rows, cols = pos
    d = x.shape[-1]
    return jnp.concatenate([rope_1d(x[..., : d // 2], rows), rope_1d(x[..., d // 2:], cols)], axis=-1)


def softmax_sink(s, sink):
    if sink is None:
        return jax.nn.softmax(s, axis=-1)
    sk = sink.astype(F32).reshape(1, s.shape[1], s.shape[2], 1, 1)
    m = jnp.maximum(s.max(axis=-1, keepdims=True), sk)
    e = jnp.exp(s - m)
    return e / (e.sum(axis=-1, keepdims=True) + jnp.exp(sk - m))


def block_attention(q, k, v, sink):
    B, Lq, H, D = q.shape
    KVH = k.shape[2]
    G = H // KVH
    E = v.shape[-1]
    NB = Lq // QBLK
    scale = D ** -0.5
    qb = q.reshape(B, NB, QBLK, KVH, G, D).transpose(1, 0, 2, 3, 4, 5)

    def one(q_blk):
        s = jnp.einsum('bqhgd,bkhd->bhgqk', q_blk, k).astype(F32) * scale
        pr = softmax_sink(s, sink).astype(v.dtype)
        return jnp.einsum('bhgqk,bkhe->bqhge', pr, v)

    o = lax.map(one, qb)
    return o.transpose(1, 0, 2, 3, 4, 5).reshape(B, Lq, H, E)


def window_attention(q, k, v, kc, vc, sink):
    B, L, H, D = q.shape
    KVH = k.shape[2]
    G = H // KVH
    NB = L // QBLK
    W3 = 3 * QBLK
    P = kc.shape[1]
    scale = D ** -0.5
    qb = q.reshape(B, NB, QBLK, KVH, G, D).transpose(1, 0, 2, 3, 4, 5)

    def band(x):
        xp = jnp.pad(x, ((0, 0), (QBLK, QBLK), (0, 0), (0, 0))).reshape(B, NB + 2, QBLK, KVH, x.shape[-1])
        xb = jnp.concatenate([xp[:, :-2], xp[:, 1:-1], xp[:, 2:]], axis=2)
        return xb.transpose(1, 0, 2, 3, 4)

    kb, vb = band(k), band(v)
    q_idx = jnp.arange(QBLK)[:, None]
    k_idx = jnp.arange(W3)[None, :]
    in_window = jnp.abs(k_idx - QBLK - q_idx) <= WINDOW
    kpos = (jnp.arange(NB)[:, None] - 1) * QBLK + jnp.arange(W3)[None, :]
    in_range = (kpos >= 0) & (kpos < L)
    mask = in_window[None, :, :] & in_range[:, None, :]

    def one(args):
        q_blk, k_blk, v_blk, m_blk = args
        s_ctx = jnp.einsum('bqhgd,bkhd->bhgqk', q_blk, kc).astype(F32) * scale
        s_loc = jnp.einsum('bqhgd,bkhd->bhgqk', q_blk, k_blk).astype(F32) * scale
        s_loc = jnp.where(m_blk, s_loc, -jnp.inf)
        pr = softmax_sink(jnp.concatenate([s_ctx, s_loc], axis=-1), sink).astype(v.dtype)
        return (jnp.einsum('bhgqk,bkhe->bqhge', pr[..., :P], vc)
                + jnp.einsum('bhgqk,bkhe->bqhge', pr[..., P:], v_blk))

    o = lax.map(one, (qb, kb, vb, mask))
    return o.transpose(1, 0, 2, 3, 4, 5).reshape(B, L, H, v.shape[-1])


def retention_scan(q, k, v, log_g, state0):
    B, L, H, DK = q.shape
    DV = v.shape[-1]
    C = RET_CHUNK
    N = L // C
    qc = q.reshape(B, N, C, H, DK)
    kc = k.reshape(B, N, C, H, DK)
    vc = v.reshape(B, N, C, H, DV)
    pos = jnp.arange(C, dtype=F32)
    rel = pos[:, None] - pos[None, :]
    causal = rel >= 0
    dmask = jnp.where(causal[None], jnp.exp(jnp.where(causal, rel, 0.0)[None] * log_g[:, None, None]), 0.0)
    s = jnp.einsum('bnqhd,bnkhd->bnhqk', qc, kc) * dmask.astype(q.dtype)
    intra = jnp.einsum('bnhqk,bnkhe->bnqhe', s, vc)
    k_w = jnp.exp((C - 1 - pos)[:, None] * log_g[None, :]).astype(k.dtype)
    U = jnp.einsum('bnkhd,kh,bnkhe->nbhde', kc, k_w, vc).astype(F32)
    g_chunk = jnp.exp(C * log_g)[None, :, None, None]

    def step(R, u):
        return g_chunk * R + u, R

    R_fin, R_prev = lax.scan(step, state0.astype(F32), U)
    q_w = jnp.exp((pos + 1.0)[:, None] * log_g[None, :]).astype(q.dtype)
    inter = jnp.einsum('bnqhd,qh,nbhde->bnqhe', qc, q_w, R_prev.astype(q.dtype))
    return (intra + inter).reshape(B, L, H, DV), R_fin.astype(v.dtype)


def bidir_retention(q, k, v, log_g2, sf, sb):
    of, rf = retention_scan(q, k, v, log_g2[0], sf)
    ob, rb = retention_scan(q[:, ::-1], k[:, ::-1], v[:, ::-1], log_g2[1], sb)
    return of + ob[:, ::-1], rf, rb


def head_group_norm(y, g):
    yf = y.astype(F32)
    mu = yf.mean(axis=-1, keepdims=True)
    var = jnp.mean(jnp.square(yf - mu), axis=-1, keepdims=True)
    out = (yf - mu) * lax.rsqrt(var + EPS) * g.astype(F32).reshape(y.shape[2], y.shape[3])
    return out.astype(y.dtype)


def mla_kv(ckv, krope, w_ukv, k_head_g):
    B, L, _ = ckv.shape
    kv = (ckv @ w_ukv).reshape(B, L, MLA_HEADS, MLA_NOPE + MLA_V)
    k_nope, v = kv[..., :MLA_NOPE], kv[..., MLA_NOPE:]
    k = jnp.concatenate([k_nope, jnp.broadcast_to(krope[:, :, None, :], (B, L, MLA_HEADS, MLA_ROPE))], axis=-1)
    return rms_norm(k, k_head_g), v


def rope_tail(x, pos):
    return jnp.concatenate([x[..., :MLA_NOPE], axial_rope(x[..., MLA_NOPE:], pos)], axis=-1)


def mixer_ab(h, p, ctx, pos):
    B, L, _ = h.shape
    cq, ckv, krope, g_a, rq, rk, rv, g_b = split_cols(h @ p['w_in'], AB_SIZES)
    ckv = rms_norm(ckv, p['kv_norm_g'])
    q = (rms_norm(cq, p['q_norm_g']) @ p['w_uq']).reshape(B, L, MLA_HEADS, MLA_QK)
    q = rms_norm(q, p['q_head_g'])
    k, v = mla_kv(ckv, krope, p['w_ukv'], p['k_head_g'])
    rq = rq.reshape(B, L, RET_HEADS, RET_DK) * (RET_DK ** -0.5)
    rk = rk.reshape(B, L, RET_HEADS, RET_DK)
    rv = rv.reshape(B, L, RET_HEADS, RET_DV)
    log_g = -jnp.exp(p['decay'].astype(F32))
    if ctx is None:
        a = block_attention(q, k, v, None)
        sf = jnp.zeros((B, RET_HEADS, RET_DK, RET_DV), F32)
        sb = sf
    else:
        ckv_c, krope_c, sf, sb = ctx
        kc, vc = mla_kv(ckv_c, krope_c, p['w_ukv'], p['k_head_g'])
        a = block_attention(rope_tail(q, pos), jnp.concatenate([kc, rope_tail(k, pos)], axis=1),
                            jnp.concatenate([vc, v], axis=1), None)
        rq = axial_rope(rq, pos)
        rk = axial_rope(rk, pos)
    r, rf, rb = bidir_retention(rq, rk, rv, log_g, sf, sb)
    r = head_group_norm(r, p['ret_norm_g'])
    mixed = jnp.concatenate([a.reshape(B, L, MLA_WIDTH) * jax.nn.silu(g_a),
                             r.reshape(B, L, RET_WIDTH) * jax.nn.silu(g_b)], axis=-1)
    return mixed @ p['w_out'], (ckv, krope, rf, rb)


def mixer_win(h, p, ctx, pos):
    B, L, _ = h.shape
    q, k, v, g = split_cols(h @ p['w_in'], WIN_SIZES)
    q = rms_norm(q.reshape(B, L, WIN_HEADS, WIN_HEAD_DIM), p['q_head_g'])
    k = rms_norm(k.reshape(B, L, WIN_KV_HEADS, WIN_HEAD_DIM), p['k_head_g'])
    v = v.reshape(B, L, WIN_KV_HEADS, WIN_HEAD_DIM)
    if ctx is None:
        o = block_attention(q, k, v, p['sink'])
    else:
        kc, vc = ctx
        o = window_attention(axial_rope(q, pos), axial_rope(k, pos), v, kc, vc, p['sink'])
    return (o.reshape(B, L, WIN_WIDTH) * jax.nn.silu(g)) @ p['w_out'], (k, v)


def ada_modulate(x, cond, w, b, g):
    mod = jax.nn.silu(cond) @ w + b
    shift, scale, gate = jnp.split(mod, 3, axis=-1)
    h = rms_norm(x, g) * (1.0 + scale[:, None, :]) + shift[:, None, :]
    return h, gate[:, None, :]


def setup_inputs(seed: int = 0) -> dict:
    key = jax.random.key(seed)
    ks = iter(jax.random.split(key, 48))

    def nrm(shape, scale):
        return scale * jax.random.normal(next(ks), shape, jnp.float32)

    D = D_MODEL
    inp = {}
    inp['x_prompt'] = nrm((BATCH, SEQ, D), 1.0)
    inp['x_sample'] = nrm((DEC_BATCH, DEC_SEQ, D), 1.0)
    for l in range(DEPTH):
        if l % 2 == 0:
            inp['cache_l%d_mla_ckv' % l] = nrm((DEC_BATCH, PAST_LEN, KV_LORA), 1.0)
            inp['cache_l%d_mla_krope' % l] = nrm((DEC_BATCH, PAST_LEN, MLA_ROPE), 1.0)
            inp['state_l%d_ret_fwd' % l] = nrm((DEC_BATCH, RET_HEADS, RET_DK, RET_DV), 1.0)
            inp['state_l%d_ret_bwd' % l] = nrm((DEC_BATCH, RET_HEADS, RET_DK, RET_DV), 1.0)
        else:
            inp['cache_l%d_win_k' % l] = nrm((DEC_BATCH, PAST_LEN, WIN_KV_HEADS, WIN_HEAD_DIM), 1.0)
            inp['cache_l%d_win_v' % l] = nrm((DEC_BATCH, PAST_LEN, WIN_KV_HEADS, WIN_HEAD_DIM), 1.0)
    inp['c'] = nrm((DEC_BATCH, D), 1.0)
    inp['c_ctx'] = nrm((D,), 1.0)
    inp['ada_w'] = nrm((DEPTH, D, 3 * D), 0.5 * D ** -0.5)
    inp['ada_b'] = nrm((DEPTH, 3 * D), 0.02)
    inp['norm_g'] = 1.0 + nrm((DEPTH, D), 0.02)
    inp['ab_w_in'] = nrm((N_EVEN, D, AB_IN), D ** -0.5)
    inp['mla_q_norm_g'] = 1.0 + nrm((N_EVEN, Q_LORA), 0.02)
    inp['mla_w_uq'] = nrm((N_EVEN, Q_LORA, MLA_HEADS * MLA_QK), Q_LORA ** -0.5)
    inp['mla_kv_norm_g'] = 1.0 + nrm((N_EVEN, KV_LORA), 0.02)
    inp['mla_w_ukv'] = nrm((N_EVEN, KV_LORA, MLA_HEADS * (MLA_NOPE + MLA_V)), KV_LORA ** -0.5)
    inp['mla_q_head_g'] = 1.0 + nrm((N_EVEN, MLA_QK), 0.02)
    inp['mla_k_head_g'] = 1.0 + nrm((N_EVEN, MLA_QK), 0.02)
    hh = np.arange(RET_HEADS)
    base = np.stack([np.log(-np.log(1.0 - 2.0 ** (-5.0 - hh))),
                     np.log(-np.log(1.0 - 2.0 ** (-5.5 - hh)))])
    inp['ret_decay'] = jnp.asarray(base, jnp.float32)[None] + nrm((N_EVEN, 2, RET_HEADS), 0.05)
    inp['ret_norm_g'] = 1.0 + nrm((N_EVEN, RET_WIDTH), 0.02)
    inp['ab_w_out'] = nrm((N_EVEN, MLA_WIDTH + RET_WIDTH, D), (MLA_WIDTH + RET_WIDTH) ** -0.5)
    inp['win_w_in'] = nrm((N_ODD, D, WIN_IN), D ** -0.5)
    inp['win_q_head_g'] = 1.0 + nrm((N_ODD, WIN_HEAD_DIM), 0.02)
    inp['win_k_head_g'] = 1.0 + nrm((N_ODD, WIN_HEAD_DIM), 0.02)
    inp['win_sink'] = nrm((N_ODD, WIN_HEADS), 0.5)
    inp['win_w_out'] = nrm((N_ODD, WIN_WIDTH, D), WIN_WIDTH ** -0.5)
    return inp


def reference(x_prompt, x_sample,
              cache_l0_mla_ckv, cache_l0_mla_krope, state_l0_ret_fwd, state_l0_ret_bwd,
              cache_l1_win_k, cache_l1_win_v,
              cache_l2_mla_ckv, cache_l2_mla_krope, state_l2_ret_fwd, state_l2_ret_bwd,
              cache_l3_win_k, cache_l3_win_v,
              c, c_ctx, ada_w, ada_b, norm_g,
              ab_w_in, mla_q_norm_g, mla_w_uq, mla_kv_norm_g, mla_w_ukv, mla_q_head_g, mla_k_head_g,
              ret_decay, ret_norm_g, ab_w_out,
              win_w_in, win_q_head_g, win_k_head_g, win_sink, win_w_out):
    pos = grid_positions(x_sample.shape[1])
    ctx_caches = ((cache_l0_mla_ckv, cache_l0_mla_krope, state_l0_ret_fwd, state_l0_ret_bwd),
                  (cache_l1_win_k, cache_l1_win_v),
                  (cache_l2_mla_ckv, cache_l2_mla_krope, state_l2_ret_fwd, state_l2_ret_bwd),
                  (cache_l3_win_k, cache_l3_win_v))
    cond_ctx = c_ctx[None, :]
    y_p, y_s = x_prompt, x_sample
    new_state = []
    for l in range(DEPTH):
        i = l // 2
        if l % 2 == 0:
            p = {'w_in': ab_w_in[i], 'q_norm_g': mla_q_norm_g[i], 'w_uq': mla_w_uq[i],
                 'kv_norm_g': mla_kv_norm_g[i], 'w_ukv': mla_w_ukv[i], 'q_head_g': mla_q_head_g[i],
                 'k_head_g': mla_k_head_g[i], 'decay': ret_decay[i], 'ret_norm_g': ret_norm_g[i],
                 'w_out': ab_w_out[i]}
            mixer = mixer_ab
        else:
            p = {'w_in': win_w_in[i], 'q_head_g': win_q_head_g[i], 'k_head_g': win_k_head_g[i],
                 'sink': win_sink[i], 'w_out': win_w_out[i]}
            mixer = mixer_win
        h_p, gate_p = ada_modulate(y_p, cond_ctx, ada_w[l], ada_b[l], norm_g[l])
        out_p, st = mixer(h_p, p, None, None)
        h_s, gate_s = ada_modulate(y_s, c, ada_w[l], ada_b[l], norm_g[l])
        out_s, _ = mixer(h_s, p, ctx_caches[l], pos)
        y_p = y_p + gate_p * out_p
        y_s = y_s + gate_s * out_s
        new_state.append(st)
    (l0_ckv, l0_krope, l0_rf, l0_rb), (l1_k, l1_v), (l2_ckv, l2_krope, l2_rf, l2_rb), (l3_k, l3_v) = new_state
    return (y_p, y_s, l0_ckv, l0_krope, l0_rf, l0_rb, l1_k, l1_v, l2_ckv, l2_krope, l2_rf, l2_rb, l3_k, l3_v)
```

```python
import contextlib
import os
import numpy as np
import ml_dtypes
import concourse.bass as bass
import concourse.mybir as mybir
from concourse.bass_utils import run_bass_kernel_spmd

F32 = mybir.dt.float32
BF16 = mybir.dt.bfloat16
ALU = mybir.AluOpType
AF = mybir.ActivationFunctionType
NPBF = ml_dtypes.bfloat16

EPOCH = 24000
EPS = 1e-6
SHIFT = 10.0
NT = 5120
NK = 5632
NG = 10
DEPTH = 4
NLAYERS = int(os.environ.get('K_NLAYERS', '4'))
KSTOP = int(os.environ.get('K_STOP', '99'))
KSUB = int(os.environ.get('K_SUB', '99'))


class Buf:
    __slots__ = ("name", "w", "r", "excl")

    def __init__(self, name=""):
        self.name = name
        self.w = None
        self.r = {}
        self.excl = False


class Op:
    __slots__ = ("eng", "fn", "deps", "dma", "needs_inc", "tok")


class Sched:
    ENGS = ("pe", "act", "dve", "pool", "sp")
    NSLOT = 24

    def __init__(self):
        self.streams = {e: [] for e in self.ENGS}
        self.slot_last = {}
        self.slot_cnt = {}
        self.cls_n = {}

    def op(self, eng, fn, reads=(), writes=(), dma=None):
        o = Op()
        o.eng, o.fn, o.dma = eng, fn, dma
        o.needs_inc = dma is not None
        o.tok = None
        deps, seen = [], set()

        def add(p):
            if p is None or id(p) in seen:
                return
            if eng == "pe" and dma is None and p.eng == "pe" and p.dma is None:
                return
            seen.add(id(p))
            deps.append(p)

        for b in reads:
            add(b.w)
            if b.excl:
                for kk, p in b.r.items():
                    if kk != ("e", eng):
                        add(p)
        for b in writes:
            add(b.w)
            for p in b.r.values():
                add(p)
        if dma is not None:
            n = self.cls_n.get(dma, 0)
            self.cls_n[dma] = n + 1
            slot = (dma, n % self.NSLOT)
            add(self.slot_last.get(slot))
            c = self.slot_cnt.get(slot, 0) + 1
            self.slot_cnt[slot] = c
            self.slot_last[slot] = o
            o.tok = (("d",) + slot, 16 * c)
            okey = ("d",) + slot
        else:
            okey = ("e", eng)
        o.deps = deps
        for p in deps:
            p.needs_inc = True
        for b in writes:
            b.w = o
            b.r = {}
        for b in reads:
            if b.w is not o:
                b.r[okey] = o
        self.streams[eng].append(o)
        return o

    def frontier(self):
        fr = {}
        for e, s in self.streams.items():
            for o in reversed(s):
                if o.dma is None:
                    fr[("e", e)] = o
                    break
        for slot, o in self.slot_last.items():
            fr[("d",) + slot] = o
        return fr

    def store_finals(self, cls="st"):
        return [o for slot, o in self.slot_last.items() if slot[0] == cls]

    def emit(self, nc, finals):
        for e in self.ENGS:
            cnt = 0
            for o in self.streams[e]:
                if o.dma is None and o.needs_inc:
                    ep = cnt // EPOCH
                    cnt += 1
                    o.tok = (("e", e, ep), cnt - ep * EPOCH)
        keys = []
        for e in self.ENGS:
            for o in self.streams[e]:
                if o.needs_inc and o.tok[0] not in keys:
                    keys.append(o.tok[0])
        with contextlib.ExitStack() as st:
            sems = {k: st.enter_context(nc.semaphore("s_" + "_".join(str(x) for x in k))) for k in keys}
            block = st.enter_context(nc.Block())
            engobj = {"pe": block.tensor, "act": block.scalar, "dve": block.vector,
                      "pool": block.gpsimd, "sp": block.sync}

            def make(e):
                ops = self.streams[e]

                def body(eng):
                    waited = {}

                    def wait_for(plist):
                        need = {}
                        for p in plist:
                            k, v = p.tok
                            if waited.get(k, 0) >= v:
                                continue
                            if need.get(k, 0) < v:
                                need[k] = v
                        for k, v in need.items():
                            eng.wait_ge(sems[k], v)
                            waited[k] = v

                    for o in ops:
                        wait_for(o.deps)
                        ins = o.fn(eng)
                        if o.needs_inc:
                            ins.then_inc(sems[o.tok[0]], 16 if o.dma is not None else 1)
                    if e == "sp":
                        wait_for(finals)
                return body

            for e in self.ENGS:
                if self.streams[e] or e == "sp":
                    engobj[e](make(e))


class T:
    def __init__(self, ap, name=""):
        self.ap = ap
        self.b = Buf(name)

    def __getitem__(self, idx):
        return self.ap[idx]


class LV:
    def __init__(self, t):
        self.t = t
        self.b = t.b
        self.cur = 0

    def __getitem__(self, idx):
        if not isinstance(idx, tuple):
            idx = (idx,)
        return self.t[(idx[0], self.cur) + tuple(idx[1:])]


class Ring:
    def __init__(self, items):
        self.items = items
        self.i = 0

    def next(self):
        t = self.items[self.i % len(self.items)]
        self.i += 1
        return t


def _rope_tables(half_dims):
    pass


def _build_consts():
    C = {}
    t = np.arange(4096)
    rows = (t // 64).astype(np.float32)
    cols = (t % 64).astype(np.float32)

    def rope_block(half, pos):
        inv = np.power(np.float32(10000.0), -np.arange(half, dtype=np.float32) / np.float32(half)).astype(np.float32)
        ang = (pos[None, :] * inv[:, None]).astype(np.float32)
        c, s = np.cos(ang).astype(np.float32), np.sin(ang).astype(np.float32)
        cosb = np.concatenate([c, c], 0)
        sinb = np.concatenate([-s, s], 0)
        return cosb, sinb

    c1, s1 = rope_block(8, rows)
    c2, s2 = rope_block(8, cols)
    cosA = np.concatenate([np.ones((64, 4096), np.float32), c1, c2], 0)
    sinA = np.concatenate([np.zeros((64, 4096), np.float32), s1, s2], 0)
    c1, s1 = rope_block(16, rows)
    c2, s2 = rope_block(16, cols)
    cb = np.concatenate([c1, c2], 0)
    sb = np.concatenate([s1, s2], 0)
    cosB = np.concatenate([cb, cb], 0)
    sinB = np.concatenate([sb, sb], 0)
    C["ropeA"] = np.ascontiguousarray(np.stack([cosA, sinA]))
    C["ropeB"] = np.ascontiguousarray(np.stack([cosB, sinB]))

    def perm(n, blocks):
        P = np.zeros((128, 128), np.float32)
        for (base, half) in blocks:
            for j in range(half):
                P[base + half + j, base + j] = 1.0
                P[base + j, base + half + j] = 1.0
        return P

    permA = perm(96, [(64, 8), (80, 8)])
    permB = perm(128, [(0, 16), (32, 16), (64, 16), (96, 16)])
    ident = np.eye(128, dtype=np.float32)
    ones = np.ones((128, 128), np.float32)
    ones64 = np.zeros((128, 128), np.float32)
    ones64[0:64, 0:64] = 1
    ones64[64:128, 64:128] = 1
    wk3 = np.zeros((128, 128), np.float32)
    for j in range(32):
        wk3[j, 64 + j] = 1.0
    k = np.arange(128)[:, None]
    q = np.arange(128)[None, :]
    mprev = np.tile((k >= q).astype(np.float32), (1, 4))
    mnext = np.tile((k <= q).astype(np.float32), (1, 4))
    cbf = np.concatenate([permA, permB, ident, ones, ones64, wk3, mprev, mnext], 1)
    C["cbf"] = cbf.astype(NPBF)
    RPf = np.maximum(q - k, 0).astype(np.float32)
    RPb = np.maximum(k - q, 0).astype(np.float32)
    INDf = (q >= k).astype(np.float32)
    INDb = (k >= q).astype(np.float32)
    QP1 = np.broadcast_to((q + 1).astype(np.float32), (128, 128))
    CMQ = np.broadcast_to((128 - q).astype(np.float32), (128, 128))
    colsm = np.zeros((128, 128), np.float32)
    colsm[:, 0] = 127 - np.arange(128)
    colsm[:, 1] = np.arange(128)
    colsm[:, 2] = EPS
    colsm[:, 3] = -SHIFT
    colsm[:, 4] = 1.0
    e64 = np.zeros((128, 128), np.float32)
    e64[:, 64] = 1.0
    C["cf32"] = np.ascontiguousarray(np.concatenate([RPf, RPb, INDf, INDb, QP1, CMQ, colsm, e64, ones], 1))
    return C


_CONSTS = None

CB_PERMA, CB_PERMB, CB_IDENT, CB_ONES, CB_ONES64, CB_WK3, CB_MPREV, CB_MNEXT = 0, 128, 256, 384, 512, 640, 768, 1280
CB_N = 1792
CF_RPF, CF_RPB, CF_INDF, CF_INDB, CF_QP1, CF_CMQ, CF_COLS, CF_E64, CF_ONES = [128 * i for i in range(9)]
CF_N = 128 * 9
PE_QNG, PE_KVNG, PE_QHG, PE_KHG, PE_RNG, PE_NORMG, PE_ADAB = 0, 3, 5, 6, 7, 11, 19
PE_N = 43
PO_Q64, PO_K64, PO_NORMG, PO_ADAB = 0, 1, 2, 10
PO_N = 34


class KB:
    def __init__(self):
        self.nc = bass.Bass("TRN2", target_bir_lowering=False)
        self.S = Sched()
        self.st = contextlib.ExitStack()
        self.finals = []
        self.hooks = {}

    def dram_in(self, name, shape, dt=F32):
        return self.nc.dram_tensor(name, list(shape), dt, kind="ExternalInput").ap()

    def dram_out(self, name, shape, dt=F32):
        return self.nc.dram_tensor(name, list(shape), dt, kind="ExternalOutput").ap()

    def dram_scr(self, name, shape, dt):
        return T(self.nc.dram_tensor(name, list(shape), dt).ap(), name)

    def sb(self, name, shape, dt):
        return T(self.st.enter_context(self.nc.sbuf_tensor("sb_" + name, list(shape), dt)), name)

    def phase(self, tag=None):
        self.aoff = 0
        self.front = self.S.frontier()
        if tag is not None:
            for f in self.hooks.pop(tag, []):
                f()

    def carve(self, cols, dt, name="", shape3=None):
        n32 = (cols * (2 if dt == BF16 else 4) + 3) // 4
        n32 = (n32 + 7) // 8 * 8
        assert self.aoff + n32 <= self.acols, ("arena overflow", name, self.aoff, n32, self.acols)
        ap = self.arena[:, self.aoff:self.aoff + n32]
        self.aoff += n32
        if dt == BF16:
            ap = ap.bitcast(BF16)
        ap = ap[:, 0:cols]
        if shape3 is not None:
            ap = ap.rearrange("p (c n) -> p c n", c=shape3)
        t = T(ap, name)
        t.b.r = dict(self.front)
        return t

    def ring(self, n, cols, dt, name, shape3=None):
        return Ring([self.carve(cols, dt, f"{name}{i}", shape3) for i in range(n)])

    def mm(self, out, lhsT, rhs, start, stop, reads, writes):
        self.S.op("pe", lambda e: e.matmul(out, lhsT=lhsT, rhs=rhs, start=start, stop=stop),
                  reads=[t.b for t in reads], writes=[t.b for t in writes])

    def transpose(self, out, in_, ident, reads, writes):
        self.S.op("pe", lambda e: e.transpose(out, in_, ident),
                  reads=[t.b for t in reads], writes=[t.b for t in writes])

    def act(self, out, in_, func, reads, writes, bias=None, scale=None):
        kw = {}
        if bias is not None:
            kw["bias"] = bias
        if scale is not None:
            kw["scale"] = scale
        self.S.op("act", lambda e: e.activation(out=out, in_=in_, func=func, **kw),
                  reads=[t.b for t in reads], writes=[t.b for t in writes])

    def stt(self, out, in0, scalar, in1, op0, op1, reads, writes, eng="dve"):
        self.S.op(eng, lambda e: e.scalar_tensor_tensor(out=out, in0=in0, scalar=scalar, in1=in1, op0=op0, op1=op1),
                  reads=[t.b for t in reads], writes=[t.b for t in writes])

    def tt(self, out, in0, in1, op, reads, writes, eng="dve"):
        self.S.op(eng, lambda e: e.tensor_tensor(out=out, in0=in0, in1=in1, op=op),
                  reads=[t.b for t in reads], writes=[t.b for t in writes])

    def ts(self, out, in0, s1, s2, op0, op1, reads, writes, eng="dve"):
        if s2 is None:
            self.S.op(eng, lambda e: e.tensor_scalar(out=out, in0=in0, scalar1=s1, scalar2=None, op0=op0),
                      reads=[t.b for t in reads], writes=[t.b for t in writes])
        else:
            self.S.op(eng, lambda e: e.tensor_scalar(out=out, in0=in0, scalar1=s1, scalar2=s2, op0=op0, op1=op1),
                      reads=[t.b for t in reads], writes=[t.b for t in writes])

    def copy(self, out, in_, reads, writes, eng="dve"):
        if eng == "act":
            self.S.op("act", lambda e: e.copy(out=out, in_=in_), reads=[t.b for t in reads], writes=[t.b for t in writes])
        else:
            self.S.op(eng, lambda e: e.tensor_copy(out=out, in_=in_), reads=[t.b for t in reads], writes=[t.b for t in writes])

    def recip(self, out, in_, reads, writes):
        def f(e):
            with self.nc.allow_low_precision(reason="bf16 reciprocal row feeds 1-pass broadcast matmul"):
                return e.reciprocal(out=out, in_=in_)
        self.S.op("dve", f, reads=[t.b for t in reads], writes=[t.b for t in writes])

    def memset(self, out, val, writes, eng="pool"):
        self.S.op(eng, lambda e: e.memset(out, val), writes=[t.b for t in writes])

    def load(self, out, in_, reads, writes):
        self.S.op("sp", lambda e: e.dma_start(out=out, in_=in_), reads=[t.b for t in reads],
                  writes=[t.b for t in writes], dma="ld")

    def load2(self, out, in_, reads, writes):
        self.S.op("pool", lambda e: e.dma_start(out=out, in_=in_), reads=[t.b for t in reads],
                  writes=[t.b for t in writes], dma="ld2")

    def store(self, out, in_, reads, writes, final=False):
        o = self.S.op("pool", lambda e: e.dma_start(out=out, in_=in_), reads=[t.b for t in reads],
                      writes=[t.b for t in writes], dma="st")
        return o

    def bank(self):
        return self.banks.next()

    def dbg(self, name, ap, t, dt=F32):
        if os.environ.get("K_DEBUG") != "1":
            return
        shp = list(ap.shape)
        d = self.nc.dram_tensor("dbg_" + name, shp, dt).ap()
        self.S.op("pool", lambda e: e.dma_start(out=d, in_=ap), reads=[t.b], dma="st")


def build():
    kb = KB()
    nc = kb.nc
    D = {}
    D["xin"] = kb.dram_in("xin", [1024, NT])
    D["cond"] = kb.dram_in("cond", [128, 16])
    D["ada_w"] = kb.dram_in("ada_w", [4, 128, 8, 3072])
    D["w_in_e"] = kb.dram_in("w_in_e", [2, 128, 8, 2720])
    D["w_in_o"] = kb.dram_in("w_in_o", [2, 128, 8, 2560])
    D["w_out"] = kb.dram_in("w_out", [4, 128, 8, 1024])
    D["w_uq"] = kb.dram_in("w_uq", [2, 128, 3, 768])
    D["wk"] = kb.dram_in("wk", [2, 128, 2, 768])
    D["wv"] = kb.dram_in("wv", [2, 128, 2, 512])
    D["pc_all"] = kb.dram_in("pc_all", [128, 4, 48])
    D["sinkrow"] = kb.dram_in("sinkrow", [2, 1, 2048])
    D["decay"] = kb.dram_in("decay", [2, 1, 8])
    D["ckvc"] = kb.dram_in("ckvc", [2, 128, 2, 512])
    D["kropec"] = kb.dram_in("kropec", [2, 32, 512])
    D["st_f"] = kb.dram_in("st_f", [2, 64, 4, 128])
    D["st_b"] = kb.dram_in("st_b", [2, 64, 4, 128])
    D["wkc"] = kb.dram_in("wkc", [2, 128, 2, 512])
    D["wvc"] = kb.dram_in("wvc", [2, 128, 4, 256])
    D["ropeA"] = kb.dram_in("ropeA", [2, 96, 4096])
    D["ropeB"] = kb.dram_in("ropeB", [2, 128, 4096])
    D["cbf"] = kb.dram_in("cbf", [128, CB_N], BF16)
    D["cf32"] = kb.dram_in("cf32", [128, CF_N])
    O = {}
    O["yT"] = kb.dram_out("yT", [1024, NT])
    for i in range(2):
        O[f"ckvT{i}"] = kb.dram_out(f"ckvT{i}", [256, 1024])
        O[f"kropeT{i}"] = kb.dram_out(f"kropeT{i}", [32, 1024])
        O[f"retf{i}"] = kb.dram_out(f"retf{i}", [4, 4, 64, 128])
        O[f"retb{i}"] = kb.dram_out(f"retb{i}", [4, 4, 64, 128])
        O[f"wkT{i}"] = kb.dram_out(f"wkT{i}", [256, 1024])
        O[f"wvo{i}"] = kb.dram_out(f"wvo{i}", [1024, 256])
    OUTT = T(None, "outs")
    XT = kb.dram_scr("XT", [1024, NT], F32)
    QT = kb.dram_scr("QT", [8, 128, NT], BF16)
    KT = kb.dram_scr("KT", [8, 128, NK], BF16)
    VA = kb.dram_scr("VA", [NK, 512], BF16)
    GT = kb.dram_scr("GT", [8, 128, NT], BF16)
    MT = kb.dram_scr("MT", [8, 128, NT], BF16)
    RQT = kb.dram_scr("RQT", [2, 128, NT], BF16)
    RKT = kb.dram_scr("RKT", [2, 128, NT], BF16)
    RV = kb.dram_scr("RV", [NT, 512], BF16)
    XIN = T(D["xin"], "xin")

    st = kb.st
    with st:
        cbf = kb.sb("cbf", [128, CB_N], BF16)
        cf = kb.sb("cf", [128, CF_N], F32)
        WIN = kb.sb("WIN", [128, 8, 2720], BF16)
        WOUT = kb.sb("WOUT", [128, 8, 1024], BF16)
        WUQ = kb.sb("WUQ", [128, 3, 768], BF16)
        WK = kb.sb("WK", [128, 2, 768], BF16)
        WV = kb.sb("WV", [128, 2, 512], BF16)
        stage = Ring([kb.sb(f"stage{i}", [128, 1360], F32) for i in range(2)])
        pc = LV(kb.sb("pc", [128, 4, 48], F32))
        modT = LV(kb.sb("modT", [128, 4, 24, 2], F32))
        gAT = LV(kb.sb("gAT", [128, 4, 8, 2], F32))

        def set_layer(l):
            pc.cur = modT.cur = gAT.cur = l
        condS = kb.sb("condS", [128, 16], F32)
        lg = kb.sb("lg", [128, 8], F32)
        rtab = kb.sb("rtab", [128, 4, 128], F32)
        qtab = kb.sb("qtab", [64, 8, 128], F32)
        rcols = kb.sb("rcols", [128, 24], F32)
        sinkB = kb.sb("sinkB", [1, 2048], BF16)
        e64b = kb.sb("e64b", [1, 128], BF16)
        kb.acols = (nc.SBUF_PARTITION_SIZE_BYTES - 1024) // 4
        used = (CB_N * 2 + CF_N * 4 + 2 * (8 * 2720 + 8 * 1024 + 3 * 768 + 2 * 768 + 2 * 512) + 2 * 1360 * 4
                + PE_N * 4 + 48 * 4 + 16 * 4 + 16 * 4 + 8 * 4 + 512 * 4 + 8 * 128 * 4 + 24 * 4 + 2048 * 4)
        kb.acols = 27400
        kb.arena = st.enter_context(nc.sbuf_tensor("arena", [128, kb.acols], F32))
        ppair_t = [st.enter_context(nc.psum_tensor(f"pp{i}", [128, 1024], F32)) for i in range(4)]
        pbanks = [T(ppair_t[i // 2][:, (i % 2) * 512:(i % 2 + 1) * 512], f"pb{i}") for i in range(8)]
        stp = Ring([(ppair_t[k][:, :], [pbanks[2 * k], pbanks[2 * k + 1]]) for k in range(3)])
        miscb = Ring(pbanks[4:6])
        for pbk in pbanks:
            pbk.b.excl = True
        kb.banks = Ring(pbanks[0:6])
        accb = Ring(pbanks[6:8])

        eps_c = cf[:, CF_COLS + 2:CF_COLS + 3]
        nshift_c = cf[:, CF_COLS + 3:CF_COLS + 4]
        ones_bf = cbf[:, CB_ONES:CB_ONES + 128]
        ones64_bf = cbf[:, CB_ONES64:CB_ONES64 + 128]

        kb.load(cbf[:, :], D["cbf"][:, :], [], [cbf])
        kb.load(cf[:, :], D["cf32"][:, :], [], [cf])
        kb.load(condS[:, :], D["cond"][:, :], [], [condS])
        kb.act(condS[:, :], condS[:, :], AF.Silu, [condS], [condS])
        kb.copy(e64b[0:1, :], cf[0:1, CF_E64:CF_E64 + 128], [cf], [e64b])

        def load_weight(dst, dst_view2d, src2d, ncols):
            c0 = 0
            while c0 < ncols:
                n = min(1360, ncols - c0)
                sg = stage.next()
                kb.load2(sg[:, 0:n], src2d[:, c0:c0 + n], [], [sg])
                kb.copy(dst_view2d[:, c0:c0 + n], sg[:, 0:n], [sg], [dst], eng="pool")
                c0 += n

        def flat(t3):
            return t3[:, :, :].rearrange("p a b -> p (a b)")

        def rstd_from(ps, rows, Dn, dst, rd):
            kb.act(dst[0:rows, :], ps[0:rows, :], AF.Ln, [ps, cf] + rd, [dst], bias=eps_c[0:rows, :], scale=1.0 / Dn)
            kb.act(dst[0:rows, :], dst[0:rows, :], AF.Exp, [dst], [dst], scale=-0.5)

        def x_norm_h(xsrc, g, j, Lr, hT=None):
            xg, rstd = Lr["xg"], Lr["rstd"]
            if hT is None:
                hT = Lr["hT"]
            kb.load(xg[:, :, :], xsrc.ap.rearrange("(c p) t -> p c t", p=128)[:, :, g * 512:(g + 1) * 512], [xsrc], [xg])
            ps = kb.bank()
            for c in range(8):
                sq = Lr["rb"].next()
                kb.act(sq[:, :], xg[:, c, :], AF.Square, [xg], [sq])
                kb.mm(ps[:, :], ones_bf, sq[:, :], c == 0, c == 7, [sq, cbf], [ps])
            rstd_from(ps, 128, 1024.0, rstd, [])
            for c in range(8):
                tf = Lr["rf"].next()
                kb.stt(tf[:, :], xg[:, c, :], gAT[:, c, j:j + 1], rstd[:, :], ALU.mult, ALU.mult, [xg, gAT, rstd], [tf])
                kb.act(hT[:, c, :], tf[:, :], AF.Identity, [tf, modT], [hT], bias=modT[:, c, j:j + 1], scale=1.0)
            return xg, hT

        def rope_apply(xf, rows, perm_ap, cosT, sinT, dst_ap, dst_t, Lr):
            xb = Lr["rb"].next()
            kb.copy(xb[0:rows, :], xf[0:rows, :], [xf], [xb], eng="act")
            ps = kb.bank()
            kb.mm(ps[0:rows, :], perm_ap[0:rows, 0:rows], xb[0:rows, :], True, True, [xb, cbf], [ps])
            t1 = Lr["rf"].next()
            kb.tt(t1[0:rows, :], xf[0:rows, :], cosT[0:rows, :], ALU.mult, [xf, cosT], [t1])
            t2 = Lr["rf"].next()
            kb.tt(t2[0:rows, :], ps[0:rows, :], sinT[0:rows, :], ALU.mult, [ps, sinT], [t2])
            kb.tt(dst_ap, t1[0:rows, :], t2[0:rows, :], ALU.add, [t1, t2], [dst_t])

        def head_norm_rope(ps, rows, Dn, ones_ap, gcol, rope, dst_ap, dst_t, Lr, prescale=None, f32_out=None):
            raw = Lr["rf"].next()
            kb.copy(raw[0:rows, :], ps[0:rows, :], [ps], [raw], eng="act")
            sq = Lr["rb"].next()
            kb.act(sq[0:rows, :], ps[0:rows, :], AF.Square, [ps], [sq])
            ps2 = kb.bank()
            kb.mm(ps2[0:rows, :], ones_ap[0:rows, 0:rows], sq[0:rows, :], True, True, [sq, cbf], [ps2])
            rs = Lr["rf"].next()
            rstd_from(ps2, rows, Dn, rs, [])
            if rope is None and f32_out is None:
                kb.stt(dst_ap, raw[0:rows, :], gcol, rs[0:rows, :], ALU.mult, ALU.mult, [raw, rs, pc], [dst_t])
                return
            xf = Lr["rf"].next() if f32_out is None else f32_out
            kb.stt(xf[0:rows, :], raw[0:rows, :], gcol, rs[0:rows, :], ALU.mult, ALU.mult, [raw, rs, pc], [xf])
            if rope is None:
                kb.copy(dst_ap, xf[0:rows, :], [xf], [dst_t], eng="act")
            else:
                rope_apply(xf, rows, rope[0], rope[1], rope[2], dst_ap, dst_t, Lr)

        def make_rings():
            R = {}
            for nm in ("raw", "rs", "xf", "t1", "t2"):
                R[nm] = kb.ring(2 if nm in ("t2", "rs", "t1") else 3, 512, F32, nm)
            R["psq"] = kb.ring(2, 512, BF16, "psq")
            R["ptf"] = kb.ring(1, 512, F32, "ptf")
            for nm in ("sq", "xb"):
                R[nm] = kb.ring(3, 512, BF16, nm)
            R["ko"] = kb.ring(4, 512, BF16, "ko")
            return R

        def run_pipeline(jobs, offs=(0, 2, 4)):
            ns = max(len(jb) for jb in jobs)
            for t in range(len(jobs) + offs[ns - 1]):
                for sidx in range(ns):
                    jj = t - offs[sidx]
                    if 0 <= jj < len(jobs) and sidx < len(jobs[jj]) and jobs[jj][sidx] is not None:
                        jobs[jj][sidx]()

        def rope_tail_stage(xf, xb, rows, rope, dst_ap, dst_t, R):
            perm_ap, cosT, sinT = rope
            ps = kb.bank()
            kb.mm(ps[0:rows, :], perm_ap[0:rows, 0:rows], xb[0:rows, :], True, True, [xb, cbf], [ps])
            t1 = R["t1"].next()
            kb.tt(t1[0:rows, :], xf[0:rows, :], cosT[0:rows, :], ALU.mult, [xf, cosT], [t1])
            t2 = R["t2"].next()
            kb.tt(t2[0:rows, :], ps[0:rows, :], sinT[0:rows, :], ALU.mult, [ps, sinT], [t2])
            kb.tt(dst_ap, t1[0:rows, :], t2[0:rows, :], ALU.add, [t1, t2], [dst_t])

        def hnr_job(mm_fn, rows, Dn, ones_ap, gcol, rope, R, post, f32_out_fn=None):
            stt_ = {}

            def A():
                ps = kb.bank()
                mm_fn(ps)
                raw = R["raw"].next()
                kb.copy(raw[0:rows, :], ps[0:rows, :], [ps], [raw], eng="act")
                sq = R["sq"].next()
                kb.act(sq[0:rows, :], ps[0:rows, :], AF.Square, [ps], [sq])
                stt_["raw"], stt_["sq"] = raw, sq

            def B():
                raw, sq = stt_["raw"], stt_["sq"]
                ps2 = kb.bank()
                kb.mm(ps2[0:rows, :], ones_ap[0:rows, 0:rows], sq[0:rows, :], True, True, [sq, cbf], [ps2])
                rs = R["rs"].next()
                rstd_from(ps2, rows, Dn, rs, [])
                ko = R["ko"].next()
                stt_["ko"] = ko
                f32_out = f32_out_fn() if f32_out_fn is not None else None
                if rope is None and f32_out is None:
                    kb.stt(ko[0:rows, :], raw[0:rows, :], gcol, rs[0:rows, :], ALU.mult, ALU.mult, [raw, rs, pc], [ko])
                    post(ko)
                    return
                xf = R["xf"].next() if f32_out is None else f32_out
                kb.stt(xf[0:rows, :], raw[0:rows, :], gcol, rs[0:rows, :], ALU.mult, ALU.mult, [raw, rs, pc], [xf])
                stt_["xf"] = xf
                if rope is None:
                    kb.copy(ko[0:rows, :], xf[0:rows, :], [xf], [ko], eng="act")
                    post(ko)
                else:
                    xb = R["xb"].next()
                    kb.copy(xb[0:rows, :], xf[0:rows, :], [xf], [xb])
                    stt_["xb"] = xb

            def C():
                if rope is None:
                    return
                ko = stt_["ko"]
                rope_tail_stage(stt_["xf"], stt_["xb"], rows, rope, ko[0:rows, :], ko, R)
                post(ko)
            return [A, B, C]

        def inproj(hT, c0, m, ps):
            for c in range(8):
                kb.mm(ps[0:m, :], WIN[:, c, c0:c0 + m], hT[:, c, :], c == 0, c == 7, [WIN, hT], [ps])

        def kbase(g):
            return g * 512 if g < 2 else 1536 + (g - 2) * 512

        class AStream:
            def __init__(self):
                self.blocks = []

            def add(self, q_fn, segs, acc, extra, post):
                self.blocks.append({"q": q_fn, "segs": segs, "acc": acc, "extra": extra, "post": post, "pre": []})
                return len(self.blocks) - 1

            def attach(self, idx, fn):
                self.blocks[max(idx, 0)]["pre"].append(fn)

            def run(self, ppairs, Lr, LA=2, DEFER=int(os.environ.get("K_DEFER", "3"))):
                units = []
                for b in self.blocks:
                    first_u = True
                    nseg = len(b["segs"])
                    for si, (c0, ncol, ktiles, rd, d, scale) in enumerate(b["segs"]):
                        nk = len(ktiles)
                        i = 0
                        while i < nk:
                            n = 2 if i + 1 < nk else 1
                            lastu = (si == nseg - 1) and (i + n == nk)
                            units.append({"b": b, "c0": c0, "ncol": ncol, "tiles": ktiles[i:i + n], "rd": rd, "scale": scale,
                                          "first": i == 0, "last": i + n == nk,
                                          "pre": b["pre"] if first_u else [], "post": b["post"] if lastu else None})
                            first_u = False
                            i += n

                def issue_s(u):
                    b, c0, ncol, tiles, rd, scale = u["b"], u["c0"], u["ncol"], u["tiles"], u["rd"], u["scale"]
                    pr = stp.next()
                    pT = ppairs.next()
                    qr = b["q"](c0, ncol)
                    n = len(tiles)
                    for j, (kT_ap, va_ap, mask_ap) in enumerate(tiles):
                        bk = pr[1][j]
                        pso = bk[:, 0:ncol]
                        if len(qr.shape) == 3:
                            pso = pso.rearrange("p (h q) -> p h q", h=qr.shape[1])
                        kb.mm(pso, kT_ap, qr, True, True, rd, [bk])
                    pv = pr[0].rearrange("p (b n) -> p b n", b=2)
                    tv = pT[:, :].rearrange("p (b n) -> p b n", b=2)
                    kb.act(tv[:, 0:n, 0:ncol], pv[:, 0:n, 0:ncol], AF.Exp, [pr[1][j] for j in range(n)] + [cf], [pT],
                           bias=nshift_c, scale=scale)
                    for j, (kT_ap, va_ap, mask_ap) in enumerate(tiles):
                        if mask_ap is not None:
                            kb.tt(pT[:, j * 512:j * 512 + ncol], pT[:, j * 512:j * 512 + ncol], mask_ap, ALU.mult, [pT, cbf], [pT])
                    return pT

                def issue_pv(u, pT):
                    b, c0, ncol, tiles, rd = u["b"], u["c0"], u["ncol"], u["tiles"], u["rd"]
                    acc, extra = b["acc"], b["extra"]
                    n = len(tiles)
                    for j, (kT_ap, va_ap, mask_ap) in enumerate(tiles):
                        st_ = u["first"] and j == 0
                        sp_ = u["last"] and j == n - 1 and extra is None
                        kb.mm(acc[0:65, c0:c0 + ncol], va_ap, pT[:, j * 512:j * 512 + ncol], st_, sp_, [pT] + rd, [acc])
                    if u["last"] and extra is not None:
                        lhsT, rhs, rdx = extra
                        kb.mm(acc[0:65, c0:c0 + ncol], lhsT, rhs, False, True, rdx, [acc])

                pend = []
                deferred = []
                nU = len(units)
                for i in range(nU + LA):
                    if i < nU:
                        u = units[i]
                        for f in u["pre"]:
                            f()
                        pend.append((u, issue_s(u)))
                    keep = []
                    for (cnt, f) in deferred:
                        if cnt <= 1:
                            f()
                        else:
                            keep.append((cnt - 1, f))
                    deferred = keep
                    if i >= LA:
                        u, pT = pend.pop(0)
                        issue_pv(u, pT)
                        if u["post"] is not None:
                            b = u["b"]
                            rr, osb = attn_finish_a(b["acc"], Lr)
                            deferred.append((DEFER, lambda rr=rr, osb=osb, post=u["post"]: post(rr, osb)))
                for (cnt, f) in deferred:
                    f()

        def attn_finish_a(acc, Lr):
            rr = Lr["rrow"].next()
            if Lr.get("act_recip"):
                ln_ = Lr["rf"].next()
                kb.act(ln_[64:65, :], acc[64:65, :], AF.Ln, [acc], [ln_])
                kb.act(rr[64:65, :], ln_[64:65, :], AF.Exp, [ln_], [rr], scale=-1.0)
            else:
                kb.recip(rr[64:65, :], acc[64:65, :], [acc], [rr])
            osb = Lr["rf"].next()
            kb.copy(osb[0:64, :], acc[0:64, :], [acc], [osb])
            return rr, osb

        def attn_finish_b(rr, osb, gt_ap, gt_t, dst_ap, dst_t, Lr, after=None):
            ps = stp.next()[1][0]
            kb.mm(ps[0:64, :], cbf[64:65, CB_ONES:CB_ONES + 64], rr[64:65, :], True, True, [rr, cbf], [ps])
            a = Lr["rf"].next()
            kb.tt(a[0:64, :], osb[0:64, :], ps[0:64, :], ALU.mult, [osb, ps], [a])
            a_ap = a[0:64, :]
            if len(dst_ap.shape) == 3:
                a_ap = a_ap.rearrange("p (h q) -> p h q", h=dst_ap.shape[1])
            kb.tt(dst_ap, a_ap, gt_ap, ALU.mult, [a, gt_t], [dst_t])
            if after is not None:
                after()

        def phase_out(l, xsrc, xdst_ap, xdst_t):
            kb.phase()
            Lr = {}
            xgs = kb.ring(2, 8 * 512, F32, "xg", 8)
            mss = kb.ring(2, 8 * 512, BF16, "ms", 8)
            outs = kb.ring(3, 512, F32, "xo")
            for g in range(NG):
                j = 0 if g < 2 else 1
                xg, ms = xgs.next(), mss.next()
                kb.load(xg[:, :, :], xsrc.ap.rearrange("(c p) t -> p c t", p=128)[:, :, g * 512:(g + 1) * 512], [xsrc], [xg])
                kb.load(ms[:, :, :], MT.ap.rearrange("c p t -> p c t")[:, :, g * 512:(g + 1) * 512], [MT], [ms])
                for m in range(8):
                    ps = kb.bank()
                    for c in range(8):
                        kb.mm(ps[:, :], WOUT[:, c, m * 128:(m + 1) * 128], ms[:, c, :], c == 0, c == 7, [WOUT, ms], [ps])
                    xo = outs.next()
                    kb.stt(xo[:, :], ps[:, :], modT[:, 16 + m, j:j + 1], xg[:, m, :], ALU.mult, ALU.add, [ps, modT, xg], [xo])
                    o = kb.store(xdst_ap[m * 128:(m + 1) * 128, g * 512:(g + 1) * 512], xo[:, :], [xo], [xdst_t])
                    if xdst_t is OUTT:
                        kb.finals.append(o)

        def phase_mod(l, pcn, normg_off, adab_off):
            kb.phase()
            ast = kb.ring(2, 8 * 512, F32, "adast", 8)
            mrow = kb.carve(3072, F32, "mrow")
            id2 = kb.carve(8, F32, "id2")
            kb.copy(id2[0:2, 0:2], cbf[0:2, CB_IDENT:CB_IDENT + 2], [cbf], [id2])
            for nck in range(6):
                sg = ast.next()
                kb.load(sg[:, :, :], D["ada_w"][l, :, :, nck * 512:(nck + 1) * 512], [], [sg])
                psr = kb.bank()
                for c in range(8):
                    kb.mm(psr[0:2, :], condS[:, 2 * c:2 * c + 2], sg[:, c, :], c == 0, c == 7, [sg, condS], [psr])
                kb.copy(mrow[0:2, nck * 512:(nck + 1) * 512], psr[0:2, :], [psr], [mrow], eng="act")
            ps = kb.bank()
            for m in range(24):
                kb.mm(ps[:, 2 * m:2 * m + 2], mrow[0:2, m * 128:(m + 1) * 128], id2[0:2, 0:2], True, True, [mrow, id2], [ps])
            psv = ps[:, 0:48].rearrange("p (m j) -> p m j", j=2)
            for j in range(2):
                kb.tt(modT[:, :, j], psv[:, :, j], pc[:, adab_off:adab_off + 24], ALU.add, [ps, pc], [modT])
                tmp = kb.carve(8, F32, "modtmp")
                kb.ts(tmp[:, :], modT[:, 8:16, j], 1.0, None, ALU.add, None, [modT], [tmp])
                kb.tt(gAT[:, :, j], tmp[:, :], pc[:, normg_off:normg_off + 8], ALU.mult, [tmp, pc], [gAT])

        def prefetch_in(l):
            if l >= NLAYERS:
                return
            i = l // 2
            if l % 2 == 0:
                load_weight(WIN, flat(WIN), D["w_in_e"][i].rearrange("p a b -> p (a b)"), 8 * 2720)
                load_weight(WUQ, flat(WUQ), D["w_uq"][i].rearrange("p a b -> p (a b)"), 3 * 768)
                load_weight(WK, flat(WK), D["wk"][i].rearrange("p a b -> p (a b)"), 2 * 768)
                load_weight(WV, flat(WV), D["wv"][i].rearrange("p a b -> p (a b)"), 2 * 512)
            else:
                load_weight(WIN, flat(WIN)[:, 0:8 * 2560], D["w_in_o"][i].rearrange("p a b -> p (a b)"), 8 * 2560)

        def prefetch_out(l):
            if l >= NLAYERS:
                return
            load_weight(WOUT, flat(WOUT), D["w_out"][l].rearrange("p a b -> p (a b)"), 8 * 1024)

        def even_layer(l, xsrc, xdst_ap, xdst_t):
            i = l // 2
            set_layer(l)
            if KSTOP <= 1:
                return
            kb.phase()
            kb.load(lg[:, :], D["decay"][i, 0:1, :].broadcast_to([128, 8]), [], [lg])
            kb.act(lg[:, :], lg[:, :], AF.Exp, [lg], [lg])
            kb.ts(lg[:, :], lg[:, :], -1.0, None, ALU.mult, None, [lg], [lg])
            tA = kb.carve(128, F32, "tA")
            tB = kb.carve(128, F32, "tB")
            for h in range(4):
                kb.act(tA[:, :], cf[:, CF_RPF:CF_RPF + 128], AF.Exp, [cf, lg], [tA], scale=lg[:, h:h + 1])
                kb.tt(tA[:, :], tA[:, :], cf[:, CF_INDF:CF_INDF + 128], ALU.mult, [tA, cf], [tA])
                kb.act(tB[:, :], cf[:, CF_RPB:CF_RPB + 128], AF.Exp, [cf, lg], [tB], scale=lg[:, 4 + h:5 + h])
                kb.tt(tB[:, :], tB[:, :], cf[:, CF_INDB:CF_INDB + 128], ALU.mult, [tB, cf], [tB])
                kb.tt(rtab[:, h, :], tA[:, :], tB[:, :], ALU.add, [tA, tB], [rtab])
                kb.act(qtab[:, h, :], cf[0:64, CF_QP1:CF_QP1 + 128], AF.Exp, [cf, lg], [qtab], scale=lg[0:64, h:h + 1])
                kb.act(qtab[:, 4 + h, :], cf[0:64, CF_CMQ:CF_CMQ + 128], AF.Exp, [cf, lg], [qtab], scale=lg[0:64, 4 + h:5 + h])
                kb.act(rcols[:, h:h + 1], cf[:, CF_COLS:CF_COLS + 1], AF.Exp, [cf, lg], [rcols], scale=lg[:, h:h + 1])
                kb.act(rcols[:, 4 + h:5 + h], cf[:, CF_COLS + 1:CF_COLS + 2], AF.Exp, [cf, lg], [rcols], scale=lg[:, 4 + h:5 + h])
            kb.act(rcols[:, 8:16], lg[:, 0:8], AF.Exp, [lg], [rcols], scale=128.0)

            if KSTOP <= 2:
                return
            kb.phase()
            Lr = {"rf": kb.ring(8, 512, F32, "rf"), "rb": kb.ring(4, 512, BF16, "rb")}
            cin = kb.carve(2 * 512, F32, "cin", 2)
            cbf16 = kb.carve(2 * 512, BF16, "cbf16", 2)
            krf = kb.carve(512, F32, "krf")
            krb = kb.carve(512, BF16, "krb")
            kouts = kb.ring(2, 512, BF16, "kout")
            vouts = kb.ring(2, 512, BF16, "vout")
            kb.load(cin[:, :, :], D["ckvc"][i], [], [cin])
            kb.load(krf[0:32, :], D["kropec"][i], [], [krf])
            kb.copy(cbf16[:, :, :], cin[:, :, :], [cin], [cbf16])
            kb.copy(krb[0:32, :], krf[0:32, :], [krf], [krb])

            def k_heads(ckvn_bf, kr_bf, rope, kcol0):
                for h in range(8):
                    ps = kb.bank()
                    kb.mm(ps[0:96, :], WK[:, 0, h * 96:(h + 1) * 96], ckvn_bf[:, 0, :], True, False, [WK, ckvn_bf], [ps])
                    kb.mm(ps[0:96, :], WK[:, 1, h * 96:(h + 1) * 96], ckvn_bf[:, 1, :], False, False, [WK, ckvn_bf], [ps])
                    kb.mm(ps[0:96, :], cbf[0:32, CB_WK3:CB_WK3 + 96], kr_bf[0:32, :], False, True, [cbf, kr_bf], [ps])
                    ko = kouts.next()
                    head_norm_rope(ps, 96, 96.0, ones_bf, pc[0:96, PE_KHG:PE_KHG + 1], rope, ko[0:96, :], ko, Lr)
                    kb.store(KT[h, 0:96, kcol0:kcol0 + 512], ko[0:96, :], [ko], [KT])

            def v_tok(ckvn_bf, krow0):
                for t in range(4):
                    ps = kb.bank()
                    kb.mm(ps[:, :], ckvn_bf[:, 0, t * 128:(t + 1) * 128], WV[:, 0, :], True, False, [ckvn_bf, WV], [ps])
                    kb.mm(ps[:, :], ckvn_bf[:, 1, t * 128:(t + 1) * 128], WV[:, 1, :], False, True, [ckvn_bf, WV], [ps])
                    vo = vouts.next()
                    kb.copy(vo[:, :], ps[:, :], [ps], [vo])
                    kb.store(VA[krow0 + t * 128:krow0 + (t + 1) * 128, :], vo[:, :], [vo], [VA])

            k_heads(cbf16, krb, None, 1024)
            v_tok(cbf16, 1024)

            if KSTOP <= 3:
                return
            kb.phase("P1")
            kb.banks = Ring(pbanks)
            R = make_rings()
            Lr = {"rf": R["ptf"], "rb": R["psq"],
                  "xg": kb.carve(8 * 512, F32, "xg", 8), "hT": kb.carve(8 * 512, BF16, "hT", 8), "hT2": kb.carve(8 * 512, BF16, "hT2", 8),
                  "rstd": kb.carve(512, F32, "rstd")}
            hTs = [Lr.get("hT"), Lr.get("hT2")]
            cq_sb = kb.carve(3 * 512, F32, "cq_sb", 3)
            cqn = kb.carve(3 * 512, BF16, "cqn", 3)
            ckv_sb = kb.carve(2 * 512, F32, "ckv_sb", 2)
            ckvn_f = kb.carve(2 * 512, F32, "ckvn_f", 2)
            ckvn_b = kb.carve(2 * 512, BF16, "ckvn_b", 2)
            kr_f = kb.carve(512, F32, "kr_f")
            kr_b = kb.carve(512, BF16, "kr_b")
            rsq = kb.carve(512, F32, "rsq")
            cosA = kb.carve(512, F32, "cosA")
            sinA = kb.carve(512, F32, "sinA")
            cosB = kb.carve(512, F32, "cosB")
            sinB = kb.carve(512, F32, "sinB")
            vouts = kb.ring(2, 512, BF16, "vout")
            for g in range(NG):
                j = 0 if g < 2 else 1
                samp = g >= 2
                t0 = g * 512
                hT = hTs[g % 2]
                if g == 0:
                    x_norm_h(xsrc, 0, 0, Lr, hT)
                ropeA = ropeB = None
                if samp:
                    s0 = (g - 2) * 512
                    kb.load(cosA[0:96, :], D["ropeA"][0, :, s0:s0 + 512], [], [cosA])
                    kb.load(sinA[0:96, :], D["ropeA"][1, :, s0:s0 + 512], [], [sinA])
                    kb.load(cosB[:, :], D["ropeB"][0, :, s0:s0 + 512], [], [cosB])
                    kb.load(sinB[:, :], D["ropeB"][1, :, s0:s0 + 512], [], [sinB])
                    ropeA = (cbf[:, CB_PERMA:CB_PERMA + 128], cosA, sinA)
                    ropeB = (cbf[:, CB_PERMB:CB_PERMB + 128], cosB, sinB)
                jobs = []

                def multi_job(nch, col0, raw_sb, Dn, gcol0, finish):
                    stt_ = {}

                    def A():
                        sqs = []
                        for c in range(nch):
                            ps = kb.bank()
                            inproj(hT, col0 + c * 128, 128, ps)
                            kb.copy(raw_sb[:, c, :], ps[:, :], [ps], [raw_sb], eng="act")
                            sq = R["sq"].next()
                            kb.act(sq[:, :], ps[:, :], AF.Square, [ps], [sq])
                            sqs.append(sq)
                        stt_["sqs"] = sqs

                    def B():
                        pss = kb.bank()
                        for c, sq in enumerate(stt_["sqs"]):
                            kb.mm(pss[:, :], ones_bf, sq[:, :], c == 0, c == nch - 1, [sq, cbf], [pss])
                        rstd_from(pss, 128, Dn, rsq, [])
                        finish()
                    return [A, B]

                def cq_fin():
                    for c in range(3):
                        kb.stt(cqn[:, c, :], cq_sb[:, c, :], pc[:, PE_QNG + c:PE_QNG + c + 1], rsq[:, :], ALU.mult, ALU.mult,
                               [cq_sb, pc, rsq], [cqn])

                def ckv_fin(samp=samp, t0=t0):
                    for c in range(2):
                        kb.stt(ckvn_f[:, c, :], ckv_sb[:, c, :], pc[:, PE_KVNG + c:PE_KVNG + c + 1], rsq[:, :], ALU.mult, ALU.mult,
                               [ckv_sb, pc, rsq], [ckvn_f])
                        kb.copy(ckvn_b[:, c, :], ckvn_f[:, c, :], [ckvn_f], [ckvn_b], eng="act")
                        if not samp:
                            o = kb.store(O[f"ckvT{i}"][c * 128:(c + 1) * 128, t0:t0 + 512], ckvn_f[:, c, :], [ckvn_f], [OUTT])
                            kb.finals.append(o)

                def krope_A(samp=samp, t0=t0):
                    ps = kb.bank()
                    inproj(hT, 640, 32, ps)
                    kb.copy(kr_f[0:32, :], ps[0:32, :], [ps], [kr_f], eng="act")
                    kb.copy(kr_b[0:32, :], kr_f[0:32, :], [kr_f], [kr_b])
                    if not samp:
                        o = kb.store(O[f"kropeT{i}"][:, t0:t0 + 512], kr_f[0:32, :], [kr_f], [OUTT])
                        kb.finals.append(o)

                def gate_A(c, t0=t0):
                    def A():
                        c0 = 672 + c * 128 if c < 4 else 2208 + (c - 4) * 128
                        ps = kb.bank()
                        inproj(hT, c0, 128, ps)
                        go = R["ko"].next()
                        kb.act(go[:, :], ps[:, :], AF.Silu, [ps], [go])
                        kb.store(GT[c, :, t0:t0 + 512], go[:, :], [go], [GT])
                    return A

                def qhead_job(h, t0=t0):
                    def mmf(ps):
                        for c in range(3):
                            kb.mm(ps[0:96, :], WUQ[:, c, h * 96:(h + 1) * 96], cqn[:, c, :], c == 0, c == 2, [WUQ, cqn], [ps])
                    return hnr_job(mmf, 96, 96.0, ones_bf, pc[0:96, PE_QHG:PE_QHG + 1], ropeA, R,
                                   post=lambda ko: kb.store(QT[h, 0:96, t0:t0 + 512], ko[0:96, :], [ko], [QT]))

                def khead_job(h, kc0):
                    def mmf(ps):
                        kb.mm(ps[0:96, :], WK[:, 0, h * 96:(h + 1) * 96], ckvn_b[:, 0, :], True, False, [WK, ckvn_b], [ps])
                        kb.mm(ps[0:96, :], WK[:, 1, h * 96:(h + 1) * 96], ckvn_b[:, 1, :], False, False, [WK, ckvn_b], [ps])
                        kb.mm(ps[0:96, :], cbf[0:32, CB_WK3:CB_WK3 + 96], kr_b[0:32, :], False, True, [cbf, kr_b], [ps])
                    return hnr_job(mmf, 96, 96.0, ones_bf, pc[0:96, PE_KHG:PE_KHG + 1], ropeA, R,
                                   post=lambda ko: kb.store(KT[h, 0:96, kc0:kc0 + 512], ko[0:96, :], [ko], [KT]))

                def vtok_A(t, kr0):
                    def A():
                        ps = kb.bank()
                        kb.mm(ps[:, :], ckvn_b[:, 0, t * 128:(t + 1) * 128], WV[:, 0, :], True, False, [ckvn_b, WV], [ps])
                        kb.mm(ps[:, :], ckvn_b[:, 1, t * 128:(t + 1) * 128], WV[:, 1, :], False, True, [ckvn_b, WV], [ps])
                        vo = vouts.next()
                        kb.copy(vo[:, :], ps[:, :], [ps], [vo])
                        kb.store(VA[kr0 + t * 128:kr0 + (t + 1) * 128, :], vo[:, :], [vo], [VA])
                    return A

                def rqk_job(c00, c, dst, scl, samp=samp, t0=t0):
                    stt_ = {}

                    def A():
                        ps = kb.bank()
                        inproj(hT, c00 + c * 128, 128, ps)
                        ko = R["ko"].next()
                        stt_["ko"] = ko
                        if not samp:
                            kb.act(ko[:, :], ps[:, :], AF.Identity, [ps], [ko], scale=scl)
                            kb.store(dst[c, :, t0:t0 + 512], ko[:, :], [ko], [dst])
                        else:
                            xf = R["xf"].next()
                            kb.act(xf[:, :], ps[:, :], AF.Identity, [ps], [xf], scale=scl)
                            xb = R["xb"].next()
                            kb.copy(xb[:, :], xf[:, :], [xf], [xb])
                            stt_["xf"], stt_["xb"] = xf, xb

                    def B():
                        if samp:
                            rope_tail_stage(stt_["xf"], stt_["xb"], 128, ropeB, stt_["ko"][:, :], stt_["ko"], R)
                            kb.store(dst[c, :, t0:t0 + 512], stt_["ko"][:, :], [stt_["ko"]], [dst])
                    return [A, B]

                def rv_A(t, t0=t0):
                    def A():
                        ps = kb.bank()
                        for c in range(8):
                            kb.mm(ps[:, :], hT[:, c, t * 128:(t + 1) * 128], WIN[:, c, 1696:2208], c == 0, c == 7, [hT, WIN], [ps])
                        vo = vouts.next()
                        kb.copy(vo[:, :], ps[:, :], [ps], [vo])
                        kb.store(RV[t0 + t * 128:t0 + (t + 1) * 128, :], vo[:, :], [vo], [RV])
                    return A

                heads_ = [qhead_job(h) for h in range(8)] + [khead_job(h, kbase(g)) for h in range(8)]
                fill_ = []
                for c00, dst, scl in ((1184, RQT, 0.125), (1440, RKT, 1.0)):
                    for c in range(2):
                        fill_.append(rqk_job(c00, c, dst, scl))
                fill_ += [[rv_A(t)] for t in range(4)]
                jobs.append(multi_job(3, 0, cq_sb, 384.0, PE_QNG, cq_fin))
                jobs.append(multi_job(2, 384, ckv_sb, 256.0, PE_KVNG, ckv_fin))
                jobs.append([krope_A])
                jobs.append(fill_[0])
                jobs.append(fill_[1])
                fi = 2
                for k_ in range(16):
                    jobs.append(heads_[k_])
                    if k_ % 2 == 1 and fi < len(fill_):
                        jobs.append(fill_[fi])
                        fi += 1
                    if k_ >= 10 and k_ % 2 == 0:
                        jobs.append([vtok_A((k_ - 10) // 2, kbase(g))]) if (k_ - 10) // 2 < 4 else None
                jobs.append([vtok_A(3, kbase(g))])
                for c in range(8):
                    jobs.append([gate_A(c)])
                if g + 1 < NG:
                    jn = 0 if g + 1 < 2 else 1
                    jobs.insert(max(len(jobs) - 9, 0), [lambda g=g, jn=jn: x_norm_h(xsrc, g + 1, jn, Lr, hTs[(g + 1) % 2])])
                run_pipeline(jobs)
            kb.hooks.setdefault("P2", []).append(lambda l=l: prefetch_in(l + 1))

            if KSTOP <= 4:
                return
            kb.banks = Ring(pbanks[0:6])
            kb.phase("P2")
            Lr = {"rf": kb.ring(12, 512, F32, "rf"), "rrow": kb.ring(6, 512, BF16, "rrow")}
            ppairs = kb.ring(4, 1024, BF16, "pT")
            kts = kb.ring(2, 4608, BF16, "kts")
            qts = kb.ring(2, 4096, BF16, "qts")
            vas = kb.ring(2, 36 * 65, BF16, "vas", 36)
            ktp = kb.ring(2, 1024, BF16, "ktp")
            qtp = kb.ring(2, 1024, BF16, "qtp")
            vap = kb.ring(2, 8 * 65, BF16, "vap", 8)
            gts = kb.ring(4, 512, BF16, "gts")
            mos = kb.ring(4, 512, BF16, "mos")
            for va in vas.items + vap.items:
                kb.memset(va[:, :, 64:65], 1.0, [va], eng="dve")
            scale = 96.0 ** -0.5
            AS = AStream()
            PF = 3
            for h in range(8):
                hb0 = len(AS.blocks)
                ktP, qtP, vaP = ktp.next(), qtp.next(), vap.next()
                kt, qt, va = kts.next(), qts.next(), vas.next()

                def head_loads(h=h, ktP=ktP, qtP=qtP, vaP=vaP, kt=kt, qt=qt, va=va):
                    kb.load(ktP[0:96, 0:1024], KT[h, 0:96, 0:1024], [KT], [ktP])
                    kb.load(qtP[0:96, 0:1024], QT[h, 0:96, 0:1024], [QT], [qtP])
                    kb.load(vaP[:, 0:8, 0:64], VA.ap[0:1024, h * 64:(h + 1) * 64].rearrange("(t p) d -> p t d", p=128), [VA], [vaP])
                    kb.load(kt[0:96, 0:4608], KT[h, 0:96, 1024:5632], [KT], [kt])
                    kb.load(qt[0:96, 0:4096], QT[h, 0:96, 1024:5120], [QT], [qt])
                    for t6 in range(6):
                        kb.load(va[:, t6 * 6:(t6 + 1) * 6, 0:64],
                                VA.ap[1024 + t6 * 768:1024 + (t6 + 1) * 768, h * 64:(h + 1) * 64].rearrange("(t p) d -> p t d", p=128), [VA], [va])
                head_pre = head_loads

                def mk_post(h, tk0):
                    gt, mo = gts.next(), mos.next()

                    def gload():
                        kb.load(gt[0:64, :], GT[h // 2, (h % 2) * 64:(h % 2) * 64 + 64, tk0:tk0 + 512], [GT], [gt])

                    def post(rr, osb):
                        attn_finish_b(rr, osb, gt[0:64, :], gt, mo[0:64, :], mo, Lr,
                                      after=lambda: kb.store(MT[h // 2, (h % 2) * 64:(h % 2) * 64 + 64, tk0:tk0 + 512], mo[0:64, :], [mo], [MT]))
                    return gload, post

                for pr in range(2):
                    acc = accb.next()
                    segs = []
                    for s2 in range(2):
                        sq_ = pr * 2 + s2
                        ktl = [(ktP[0:96, sq_ * 256 + u * 128:sq_ * 256 + (u + 1) * 128], vaP[:, sq_ * 2 + u, :], None) for u in range(2)]
                        segs.append((s2 * 256, 256, ktl, [ktP, qtP, vaP], 96, scale))
                    gload, post = mk_post(h, pr * 512)
                    bi = AS.add(lambda c0, n, qtP=qtP, pr=pr: qtP[0:96, pr * 512 + c0:pr * 512 + c0 + n], segs, acc, None, post)
                    AS.attach(bi, gload)
                for qg in range(8):
                    acc = accb.next()
                    ktl = [(kt[0:96, u * 128:(u + 1) * 128], va[:, u, :], None) for u in range(36)]
                    segs = [(0, 512, ktl, [kt, qt, va], 96, scale)]
                    gload, post = mk_post(h, 1024 + qg * 512)
                    bi = AS.add(lambda c0, n, qt=qt, qg=qg: qt[0:96, qg * 512 + c0:qg * 512 + c0 + n], segs, acc, None, post)
                    AS.attach(bi, gload)
                AS.attach(hb0 - PF, head_pre)
            AS.run(ppairs, Lr)

            if KSTOP <= 5:
                return
            kb.phase()
            Lr = {"rf": kb.ring(8, 512, F32, "rf"), "rb": kb.ring(4, 512, BF16, "rb")}
            rqs = kb.ring(2, 4096, BF16, "rqs")
            rks = kb.ring(2, 4096, BF16, "rks")
            rvs = kb.ring(2, 32 * 128, BF16, "rvs", 32)
            kwf = kb.carve(32 * 64, BF16, "kwf", 32)
            kwb = kb.carve(32 * 64, BF16, "kwb", 32)
            Rfb = kb.carve(33 * 128, BF16, "Rfb", 33)
            Rbb = kb.carve(33 * 128, BF16, "Rbb", 33)
            Rst = kb.ring(4, 128, F32, "Rst")
            smr = kb.ring(2, 512, BF16, "smr")
            qfr = kb.ring(2, 512, BF16, "qfr")
            qbr = kb.ring(2, 512, BF16, "qbr")
            gts = kb.ring(2, 512, BF16, "gts")
            mos = kb.ring(2, 512, BF16, "mos")
            ident_bf = cbf[:, CB_IDENT:CB_IDENT + 128]
            for h in range(4):
                ch, r0 = h // 2, (h % 2) * 64
                for sq_i in range(5):
                    samp = sq_i == 4
                    tk0 = 1024 if samp else sq_i * 256
                    nch = 32 if samp else 2
                    L = nch * 128
                    rq, rk, rv = rqs.next(), rks.next(), rvs.next()
                    kb.load(rq[0:64, 0:L], RQT[ch, r0:r0 + 64, tk0:tk0 + L], [RQT], [rq])
                    kb.load(rk[0:64, 0:L], RKT[ch, r0:r0 + 64, tk0:tk0 + L], [RKT], [rk])
                    for t8 in range(0, nch, 8):
                        te = min(nch, t8 + 8)
                        kb.load(rv[:, t8:te, :], RV.ap[tk0 + t8 * 128:tk0 + te * 128, h * 128:(h + 1) * 128].rearrange("(t p) d -> p t d", p=128), [RV], [rv])
                    for n0 in range(0, nch, 4):
                        nn = min(4, nch - n0)
                        ps = kb.bank()
                        for u in range(nn):
                            n = n0 + u
                            kb.mm(ps[:, u * 64:(u + 1) * 64], rk[0:64, n * 128:(n + 1) * 128], ident_bf[0:64, 0:64], True, True, [rk, cbf], [ps])
                        psv = ps[:, 0:nn * 64].rearrange("p (c d) -> p c d", c=nn)
                        kb.ts(kwf[:, n0:n0 + nn, :], psv, rcols[:, h:h + 1], None, ALU.mult, None, [ps, rcols], [kwf])
                        kb.act(kwb[:, n0:n0 + nn, :], psv, AF.Identity, [ps, rcols], [kwb], scale=rcols[:, 4 + h:5 + h])
                    if h == 0 and sq_i == 0:
                        kb.dbg("kwf", kwf[:, 0:2, :], kwf, BF16)
                        kb.dbg("kwb", kwb[:, 0:2, :], kwb, BF16)
                        kb.dbg("rk", rk[0:64, 0:256], rk, BF16)
                        kb.dbg("rq", rq[0:64, 0:256], rq, BF16)
                        kb.dbg("rv", rv[:, 0:2, :], rv, BF16)
                        kb.dbg("rcols", rcols[:, 0:16], rcols)
                        kb.dbg("rtab", rtab[:, 0, :], rtab)
                        kb.dbg("qtab", qtab[:, :, :], qtab)
                    if KSUB <= 1:
                        continue
                    Rc = Rst.next()
                    if samp:
                        kb.load(Rc[0:64, :], D["st_f"][i, :, h, :], [], [Rc])
                    else:
                        kb.memset(Rc[0:64, :], 0.0, [Rc], eng="dve")
                    for n in range(nch):
                        kb.copy(Rfb[0:64, n, :], Rc[0:64, :], [Rc], [Rfb], eng="act")
                        ps = kb.bank()
                        kb.mm(ps[0:64, 0:128], kwf[:, n, :], rv[:, n, :], True, True, [kwf, rv], [ps])
                        Rn = Rst.next()
                        kb.stt(Rn[0:64, :], Rc[0:64, :], rcols[0:64, 8 + h:9 + h], ps[0:64, 0:128], ALU.mult, ALU.add,
                               [Rc, rcols, ps], [Rn])
                        Rc = Rn
                    if not samp:
                        o = kb.store(O[f"retf{i}"][sq_i, h, :, :], Rc[0:64, :], [Rc], [OUTT])
                        kb.finals.append(o)
                    if KSUB <= 2:
                        continue
                    Rc = Rst.next()
                    if samp:
                        kb.load(Rc[0:64, :], D["st_b"][i, :, h, :], [], [Rc])
                    else:
                        kb.memset(Rc[0:64, :], 0.0, [Rc], eng="dve")
                    for n in range(nch - 1, -1, -1):
                        kb.copy(Rbb[0:64, n, :], Rc[0:64, :], [Rc], [Rbb], eng="act")
                        ps = kb.bank()
                        kb.mm(ps[0:64, 0:128], kwb[:, n, :], rv[:, n, :], True, True, [kwb, rv], [ps])
                        Rn = Rst.next()
                        kb.stt(Rn[0:64, :], Rc[0:64, :], rcols[0:64, 12 + h:13 + h], ps[0:64, 0:128], ALU.mult, ALU.add,
                               [Rc, rcols, ps], [Rn])
                        Rc = Rn
                    if not samp:
                        o = kb.store(O[f"retb{i}"][sq_i, h, :, :], Rc[0:64, :], [Rc], [OUTT])
                        kb.finals.append(o)
                    if h == 0 and sq_i == 0:
                        kb.dbg("Rfb", Rfb[0:64, 0:2, :], Rfb, BF16)
                        kb.dbg("Rbb", Rbb[0:64, 0:2, :], Rbb, BF16)
                    if KSUB <= 3:
                        continue
                    nblk = (nch + 3) // 4
                    for blk in range(nblk):
                        cpb = min(4, nch - blk * 4)
                        W = cpb * 128
                        acc = accb.next()
                        nb0 = blk * 4
                        ps = kb.bank()
                        for u in range(cpb):
                            n = nb0 + u
                            kb.mm(ps[:, u * 128:(u + 1) * 128], rk[0:64, n * 128:(n + 1) * 128], rq[0:64, n * 128:(n + 1) * 128], True, True, [rk, rq], [ps])
                        sm = smr.next()
                        kb.tt(sm[:, 0:W].rearrange("p (c q) -> p c q", c=cpb), ps[:, 0:W].rearrange("p (c q) -> p c q", c=cpb),
                              rtab[:, h:h + 1, :].broadcast_to([128, cpb, 128]), ALU.mult, [ps, rtab], [sm])
                        qf, qb_ = qfr.next(), qbr.next()
                        rqv = rq[0:64, nb0 * 128:nb0 * 128 + W].rearrange("p (c q) -> p c q", c=cpb)
                        kb.tt(qf[0:64, 0:W].rearrange("p (c q) -> p c q", c=cpb), rqv,
                              qtab[:, h:h + 1, :].broadcast_to([64, cpb, 128]), ALU.mult, [rq, qtab], [qf])
                        kb.tt(qb_[0:64, 0:W].rearrange("p (c q) -> p c q", c=cpb), rqv,
                              qtab[:, 4 + h:5 + h, :].broadcast_to([64, cpb, 128]), ALU.mult, [rq, qtab], [qb_])
                        for u in range(cpb):
                            n = nb0 + u
                            cs = slice(u * 128, (u + 1) * 128)
                            kb.mm(acc[:, cs], rv[:, n, :], sm[:, cs], True, False, [rv, sm], [acc])
                            kb.mm(acc[:, cs], Rfb[0:64, n, :], qf[0:64, cs], False, False, [Rfb, qf], [acc])
                            kb.mm(acc[:, cs], Rbb[0:64, n, :], qb_[0:64, cs], False, True, [Rbb, qb_], [acc])
                        xs = Lr["rf"].next()
                        kb.copy(xs[:, 0:W], acc[:, 0:W], [acc], [xs], eng="act")
                        xb = Lr["rb"].next()
                        kb.copy(xb[:, 0:W], acc[:, 0:W], [acc], [xb])
                        sqb = Lr["rb"].next()
                        kb.act(sqb[:, 0:W], acc[:, 0:W], AF.Square, [acc], [sqb])
                        pm, pq = kb.bank(), kb.bank()
                        kb.mm(pm[:, 0:W], ones_bf, xb[:, 0:W], True, True, [xb, cbf], [pm])
                        kb.mm(pq[:, 0:W], ones_bf, sqb[:, 0:W], True, True, [sqb, cbf], [pq])
                        mean = Lr["rf"].next()
                        kb.act(mean[:, 0:W], pm[:, 0:W], AF.Identity, [pm], [mean], scale=1.0 / 128)
                        m2 = Lr["rf"].next()
                        kb.tt(m2[:, 0:W], mean[:, 0:W], mean[:, 0:W], ALU.mult, [mean], [m2])
                        var = Lr["rf"].next()
                        kb.stt(var[:, 0:W], pq[:, 0:W], 1.0 / 128, m2[:, 0:W], ALU.mult, ALU.subtract, [pq, m2], [var])
                        rs = Lr["rf"].next()
                        kb.act(rs[:, 0:W], var[:, 0:W], AF.Ln, [var, cf], [rs], bias=eps_c, scale=1.0)
                        kb.act(rs[:, 0:W], rs[:, 0:W], AF.Exp, [rs], [rs], scale=-0.5)
                        xc = Lr["rf"].next()
                        kb.tt(xc[:, 0:W], xs[:, 0:W], mean[:, 0:W], ALU.subtract, [xs, mean], [xc])
                        y = Lr["rf"].next()
                        kb.stt(y[:, 0:W], xc[:, 0:W], pc[:, PE_RNG + h:PE_RNG + h + 1], rs[:, 0:W], ALU.mult, ALU.mult, [xc, pc, rs], [y])
                        gt, mo = gts.next(), mos.next()
                        tb0 = tk0 + blk * 512
                        kb.load(gt[:, 0:W], GT[4 + h, :, tb0:tb0 + W], [GT], [gt])
                        kb.tt(mo[:, 0:W], y[:, 0:W], gt[:, 0:W], ALU.mult, [y, gt], [mo])
                        kb.store(MT[4 + h, :, tb0:tb0 + W], mo[:, 0:W], [mo], [MT])

            if KSTOP <= 6:
                return
            phase_out(l, xsrc, xdst_ap, xdst_t)
            kb.hooks.setdefault("P1", []).append(lambda l=l: prefetch_out(l + 1))

        def odd_layer(l, xsrc, xdst_ap, xdst_t):
            i = l // 2
            set_layer(l)
            WINo = T(flat(WIN)[:, 0:8 * 2560].rearrange("p (a b) -> p a b", a=8), "WINo")
            WINo.b = WIN.b
            sinkE = kb.carve(2048, F32, "sinkE")
            kb.load(sinkE[0:1, :], D["sinkrow"][i], [], [sinkE])
            kb.act(sinkE[0:1, :], sinkE[0:1, :], AF.Exp, [sinkE, cf], [sinkE], bias=nshift_c[0:1, :], scale=1.0)
            kb.copy(sinkB[0:1, :], sinkE[0:1, :], [sinkE], [sinkB])

            def inproj_o(hT, c0, m, ps):
                for c in range(8):
                    kb.mm(ps[0:m, :], WINo[:, c, c0:c0 + m], hT[:, c, :], c == 0, c == 7, [WINo, hT], [ps])

            kb.phase()
            cin = kb.carve(2 * 512, F32, "cin", 2)
            cb2 = kb.carve(2 * 512, BF16, "cb2", 2)
            vin = kb.carve(4 * 256, F32, "vin", 4)
            vb2 = kb.carve(4 * 256, BF16, "vb2", 4)
            kb.load(cin[:, :, :], D["wkc"][i], [], [cin])
            kb.copy(cb2[:, :, :], cin[:, :, :], [cin], [cb2])
            for c in range(2):
                kb.store(KT[c, :, 1024:1536], cb2[:, c, :], [cb2], [KT])
            kb.load(vin[:, :, :], D["wvc"][i], [], [vin])
            kb.copy(vb2[:, :, :], vin[:, :, :], [vin], [vb2])
            kb.store(VA.ap[1024:1536, 0:256].rearrange("(t p) d -> p t d", p=128), vb2[:, :, :], [vb2], [VA])

            kb.phase("P1")
            kb.banks = Ring(pbanks)
            R = make_rings()
            Lr = {"rf": R["ptf"], "rb": R["psq"],
                  "xg": kb.carve(8 * 512, F32, "xg", 8), "hT": kb.carve(8 * 512, BF16, "hT", 8), "hT2": kb.carve(8 * 512, BF16, "hT2", 8),
                  "rstd": kb.carve(512, F32, "rstd")}
            hTs = [Lr.get("hT"), Lr.get("hT2")]
            cosB = kb.carve(512, F32, "cosB")
            sinB = kb.carve(512, F32, "sinB")
            kf32 = kb.ring(3, 512, F32, "kf32")
            vouts = kb.ring(3, 256, BF16, "vout")
            vf32 = kb.ring(3, 256, F32, "vf32")
            for g in range(NG):
                j = 0 if g < 2 else 1
                samp = g >= 2
                t0 = g * 512
                hT = hTs[g % 2]
                if g == 0:
                    x_norm_h(xsrc, 0, 0, Lr, hT)
                ropeB = None
                if samp:
                    s0 = (g - 2) * 512
                    kb.load(cosB[:, :], D["ropeB"][0, :, s0:s0 + 512], [], [cosB])
                    kb.load(sinB[:, :], D["ropeB"][1, :, s0:s0 + 512], [], [sinB])
                    ropeB = (cbf[:, CB_PERMB:CB_PERMB + 128], cosB, sinB)
                jobs = []

                def q_job(c, t0=t0):
                    return hnr_job(lambda ps: inproj_o(hT, c * 128, 128, ps), 128, 64.0, ones64_bf, pc[:, PO_Q64:PO_Q64 + 1], ropeB, R,
                                   post=lambda ko: kb.store(QT[c, :, t0:t0 + 512], ko[:, :], [ko], [QT]))

                def k_job(c, samp=samp, t0=t0, kc0=kbase(g)):
                    hold = {}

                    def f32fn():
                        hold["kf"] = kf32.next()
                        return hold["kf"]

                    def post(ko):
                        if not samp:
                            kf = hold["kf"]
                            o = kb.store(O[f"wkT{i}"][c * 128:(c + 1) * 128, t0:t0 + 512], kf[:, :], [kf], [OUTT])
                            kb.finals.append(o)
                        kb.store(KT[c, :, kc0:kc0 + 512], ko[:, :], [ko], [KT])
                    return hnr_job(lambda ps: inproj_o(hT, 1024 + c * 128, 128, ps), 128, 64.0, ones64_bf, pc[:, PO_K64:PO_K64 + 1],
                                   ropeB, R, post=post, f32_out_fn=(None if samp else f32fn))

                def v_A(t, samp=samp, t0=t0, kc0=kbase(g)):
                    def A():
                        ps = kb.bank()
                        for c in range(8):
                            kb.mm(ps[:, 0:256], hT[:, c, t * 128:(t + 1) * 128], WINo[:, c, 1280:1536], c == 0, c == 7, [hT, WINo], [ps])
                        vo = vouts.next()
                        kb.copy(vo[:, :], ps[:, 0:256], [ps], [vo])
                        kb.store(VA.ap[kc0 + t * 128:kc0 + (t + 1) * 128, 0:256], vo[:, :], [vo], [VA])
                        if not samp:
                            vf = vf32.next()
                            kb.copy(vf[:, :], ps[:, 0:256], [ps], [vf])
                            o = kb.store(O[f"wvo{i}"][t0 + t * 128:t0 + (t + 1) * 128, :], vf[:, :], [vf], [OUTT])
                            kb.finals.append(o)
                    return A

                def g_A(c, t0=t0):
                    def A():
                        ps = kb.bank()
                        inproj_o(hT, 1536 + c * 128, 128, ps)
                        go = R["ko"].next()
                        kb.act(go[:, :], ps[:, :], AF.Silu, [ps], [go])
                        kb.store(GT[c, :, t0:t0 + 512], go[:, :], [go], [GT])
                    return A

                for c in range(8):
                    jobs.append(q_job(c))
                    if c % 2 == 1:
                        jobs.append([v_A(c // 2)])
                for c in range(2):
                    jobs.append(k_job(c))
                for c in range(8):
                    jobs.append([g_A(c)])
                if g + 1 < NG:
                    jn = 0 if g + 1 < 2 else 1
                    jobs.insert(max(len(jobs) - 9, 0), [lambda g=g, jn=jn: x_norm_h(xsrc, g + 1, jn, Lr, hTs[(g + 1) % 2])])
                run_pipeline(jobs)
            kb.hooks.setdefault("P2", []).append(lambda l=l: prefetch_in(l + 1))

            kb.banks = Ring(pbanks[0:6])
            kb.phase("P2")
            Lr = {"rf": kb.ring(9, 512, F32, "rf"), "rrow": kb.ring(6, 512, BF16, "rrow"), "act_recip": True}
            ppairs = kb.ring(4, 1024, BF16, "pT")
            kts = kb.ring(2, 4608, BF16, "kts")
            vas = kb.ring(2, 36 * 65, BF16, "vas", 36)
            ktp = kb.ring(2, 1024, BF16, "ktp")
            vap = kb.ring(2, 8 * 65, BF16, "vap", 8)
            qgs = kb.ring(3, 4 * 512, BF16, "qgs", 4)
            gts = kb.ring(4, 4 * 512, BF16, "gts", 4)
            mos = kb.ring(2, 4 * 512, BF16, "mos", 4)
            for va in vas.items + vap.items:
                kb.memset(va[:, :, 64:65], 1.0, [va], eng="dve")
            for tz in kts.items + ktp.items:
                kb.memset(tz[64:128, :], 0.0, [tz], eng="dve")
            for tz in qgs.items:
                kb.memset(tz[64:128, :, :], 0.0, [tz], eng="dve")
            scale = 0.125
            mprev = cbf[:, CB_MPREV:CB_MPREV + 512]
            mnext = cbf[:, CB_MNEXT:CB_MNEXT + 512]
            e64 = cf[0:1, CF_E64:CF_E64 + 65]
            QTh = QT.ap.rearrange("c (two r) t -> (c two) r t", two=2)
            GTh = GT.ap.rearrange("c (two r) t -> (c two) r t", two=2)
            MTh = MT.ap.rearrange("c (two r) t -> (c two) r t", two=2)
            KTh = KT.ap.rearrange("c (two r) t -> (c two) r t", two=2)
            AS = AStream()
            PF = 4
            for kvh in range(4):
                sink_ap = sinkB[0:1, kvh * 512:(kvh + 1) * 512]
                extra = (e64b[0:1, 0:65], sink_ap, [e64b, sinkB])
                hb0 = len(AS.blocks)
                ktP, vaP = ktp.next(), vap.next()
                kt, va = kts.next(), vas.next()

                def head_loads(kvh=kvh, ktP=ktP, vaP=vaP, kt=kt, va=va):
                    kb.load(ktP[0:64, 0:1024], KTh[kvh, :, 0:1024], [KT], [ktP])
                    kb.load(vaP[:, 0:8, 0:64], VA.ap[0:1024, kvh * 64:(kvh + 1) * 64].rearrange("(t p) d -> p t d", p=128), [VA], [vaP])
                    kb.load(kt[0:64, 0:4608], KTh[kvh, :, 1024:5632], [KT], [kt])
                    for t6 in range(6):
                        kb.load(va[:, t6 * 6:(t6 + 1) * 6, 0:64],
                                VA.ap[1024 + t6 * 768:1024 + (t6 + 1) * 768, kvh * 64:(kvh + 1) * 64].rearrange("(t p) d -> p t d", p=128), [VA], [va])

                for ck in range(10):
                    tk0 = ck * 512
                    qg, gt, mo = qgs.next(), gts.next(), mos.next()

                    def chunk_loads(kvh=kvh, qg=qg, gt=gt, tk0=tk0):
                        kb.load(qg[0:64, :, :], QTh[kvh * 4:(kvh + 1) * 4, :, tk0:tk0 + 512].rearrange("h r t -> r h t"), [QT], [qg])
                        kb.load(gt[0:64, :, :], GTh[kvh * 4:(kvh + 1) * 4, :, tk0:tk0 + 512].rearrange("h r t -> r h t"), [GT], [gt])
                    cb0 = len(AS.blocks)
                    for qb in range(4):
                        acc = accb.next()
                        if ck < 2:
                            sq_ = ck * 2 + qb // 2
                            ktl = [(ktP[0:128, sq_ * 256 + u * 128:sq_ * 256 + (u + 1) * 128], vaP[:, sq_ * 2 + u, :], None) for u in range(2)]
                            rd = [ktP, qg, vaP]
                        else:
                            bi_ = (ck - 2) * 4 + qb
                            ktl = [(kt[0:128, u * 128:(u + 1) * 128], va[:, u, :], None) for u in range(4)]
                            if bi_ > 0:
                                u = 4 + bi_ - 1
                                ktl.append((kt[0:128, u * 128:(u + 1) * 128], va[:, u, :], mprev))
                            u = 4 + bi_
                            ktl.append((kt[0:128, u * 128:(u + 1) * 128], va[:, u, :], None))
                            if bi_ < 31:
                                u = 4 + bi_ + 1
                                ktl.append((kt[0:128, u * 128:(u + 1) * 128], va[:, u, :], mnext))
                            rd = [kt, qg, va]
                        segs = [(0, 512, ktl, rd, 64, scale)]
                        aft = None
                        if qb == 3:
                            aft = (lambda mo=mo, tk0=tk0, kvh=kvh: kb.store(
                                MTh[kvh * 4:(kvh + 1) * 4, :, tk0:tk0 + 512].rearrange("h r t -> r h t"), mo[0:64, :, :], [mo], [MT]))

                        def post(rr, osb, gt=gt, mo=mo, qb=qb, aft=aft):
                            attn_finish_b(rr, osb, gt[0:64, :, qb * 128:(qb + 1) * 128], gt, mo[0:64, :, qb * 128:(qb + 1) * 128], mo, Lr, after=aft)
                        AS.add(lambda c0, n, qg=qg, qb=qb: qg[0:128, :, qb * 128:(qb + 1) * 128], segs, acc, extra, post)
                    AS.attach(cb0 - PF, chunk_loads)
                AS.attach(hb0 - PF, head_loads)
            AS.run(ppairs, Lr)

            phase_out(l, xsrc, xdst_ap, xdst_t)
            kb.hooks.setdefault("P1", []).append(lambda l=l: prefetch_out(l + 1))

        kb.load(pc.t[:, :, :], D["pc_all"][:, :, :], [], [pc])
        prefetch_in(0)
        prefetch_out(0)
        for l in range(NLAYERS):
            set_layer(l)
            if l % 2 == 0:
                phase_mod(l, pc, PE_NORMG, PE_ADAB)
            else:
                phase_mod(l, pc, PO_NORMG, PO_ADAB)
        for l in range(NLAYERS):
            xsrc = XIN if l == 0 else XT
            if l == NLAYERS - 1:
                xdst_ap, xdst_t = O["yT"], OUTT
            else:
                xdst_ap, xdst_t = XT.ap, XT
            if l % 2 == 0:
                even_layer(l, xsrc, xdst_ap, xdst_t)
            else:
                odd_layer(l, xsrc, xdst_ap, xdst_t)
        kb.S.emit(nc, kb.S.store_finals("st"))
    return nc


def _fm(v, ncol):
    return np.ascontiguousarray(v.reshape(ncol, 128).T)


def _wl(w):
    K, N = w.shape
    return np.ascontiguousarray(w.reshape(K // 128, 128, N).transpose(1, 0, 2))


def _prep_shared(inp):
    global _CONSTS
    if _CONSTS is None:
        _CONSTS = _build_consts()
    f = np.float32
    sh = dict(_CONSTS)
    sh["ada_w"] = np.stack([_wl(np.asarray(inp["ada_w"][l], f)) for l in range(4)])
    sh["w_in_e"] = np.stack([_wl(np.asarray(inp["ab_w_in"][i], f)) for i in range(2)])
    sh["w_in_o"] = np.stack([_wl(np.asarray(inp["win_w_in"][i], f)) for i in range(2)])
    wo = [inp["ab_w_out"][0], inp["win_w_out"][0], inp["ab_w_out"][1], inp["win_w_out"][1]]
    sh["w_out"] = np.stack([_wl(np.asarray(w, f)) for w in wo])
    sh["w_uq"] = np.stack([_wl(np.asarray(inp["mla_w_uq"][i], f)) for i in range(2)])
    wk = np.zeros((2, 256, 8, 96), f)
    wv = np.zeros((2, 256, 8, 64), f)
    for i in range(2):
        w = np.asarray(inp["mla_w_ukv"][i], f).reshape(256, 8, 128)
        wk[i, :, :, 0:64] = w[:, :, 0:64]
        wv[i] = w[:, :, 64:128]
    sh["wk"] = np.stack([_wl(wk[i].reshape(256, 768)) for i in range(2)])
    sh["wv"] = np.stack([_wl(wv[i].reshape(256, 512)) for i in range(2)])
    pce = np.zeros((2, 128, PE_N), f)
    pco = np.zeros((2, 128, PO_N), f)
    for i in range(2):
        pce[i, :, PE_QNG:PE_QNG + 3] = _fm(np.asarray(inp["mla_q_norm_g"][i], f), 3)
        pce[i, :, PE_KVNG:PE_KVNG + 2] = _fm(np.asarray(inp["mla_kv_norm_g"][i], f), 2)
        pce[i, 0:96, PE_QHG] = np.asarray(inp["mla_q_head_g"][i], f)
        pce[i, 0:96, PE_KHG] = np.asarray(inp["mla_k_head_g"][i], f)
        pce[i, :, PE_RNG:PE_RNG + 4] = _fm(np.asarray(inp["ret_norm_g"][i], f), 4)
        pce[i, :, PE_NORMG:PE_NORMG + 8] = _fm(np.asarray(inp["norm_g"][2 * i], f), 8)
        pce[i, :, PE_ADAB:PE_ADAB + 24] = _fm(np.asarray(inp["ada_b"][2 * i], f), 24)
        pco[i, :, PO_Q64] = np.tile(np.asarray(inp["win_q_head_g"][i], f), 2)
        pco[i, :, PO_K64] = np.tile(np.asarray(inp["win_k_head_g"][i], f), 2)
        pco[i, :, PO_NORMG:PO_NORMG + 8] = _fm(np.asarray(inp["norm_g"][2 * i + 1], f), 8)
        pco[i, :, PO_ADAB:PO_ADAB + 24] = _fm(np.asarray(inp["ada_b"][2 * i + 1], f), 24)
    pca = np.zeros((128, 4, 48), f)
    for i in range(2):
        pca[:, 2 * i, 0:PE_N] = pce[i]
        pca[:, 2 * i + 1, 0:PO_N] = pco[i]
    sh["pc_all"] = pca
    sh["sinkrow"] = np.ascontiguousarray(np.repeat(np.asarray(inp["win_sink"], f), 128, axis=1).reshape(2, 1, 2048))
    sh["decay"] = np.ascontiguousarray(np.asarray(inp["ret_decay"], f).reshape(2, 1, 8))
    return sh


def _prep_core(inp, c):
    f = np.float32
    b = c // 2
    m = {}
    xp = np.asarray(inp["x_prompt"][4 * c:4 * c + 4], f).reshape(1024, 1024)
    xs = np.asarray(inp["x_sample"][b], f)
    m["xin"] = np.ascontiguousarray(np.concatenate([xp, xs], 0).T)
    cond = np.stack([np.asarray(inp["c_ctx"], f), np.asarray(inp["c"][b], f)], -1)
    m["cond"] = np.ascontiguousarray(cond.reshape(8, 128, 2).transpose(1, 0, 2).reshape(128, 16))
    ck = [inp["cache_l0_mla_ckv"], inp["cache_l2_mla_ckv"]]
    kr = [inp["cache_l0_mla_krope"], inp["cache_l2_mla_krope"]]
    m["ckvc"] = np.stack([_wl(np.ascontiguousarray(np.asarray(x[b], f).T)) for x in ck])
    m["kropec"] = np.stack([np.ascontiguousarray(np.asarray(x[b], f).T) for x in kr])
    sf = [inp["state_l0_ret_fwd"], inp["state_l2_ret_fwd"]]
    sb_ = [inp["state_l0_ret_bwd"], inp["state_l2_ret_bwd"]]
    m["st_f"] = np.stack([np.ascontiguousarray(np.asarray(x[b], f).transpose(1, 0, 2)) for x in sf])
    m["st_b"] = np.stack([np.ascontiguousarray(np.asarray(x[b], f).transpose(1, 0, 2)) for x in sb_])
    wk_ = [inp["cache_l1_win_k"], inp["cache_l3_win_k"]]
    wv_ = [inp["cache_l1_win_v"], inp["cache_l3_win_v"]]
    m["wkc"] = np.stack([_wl(np.ascontiguousarray(np.asarray(x[b], f).reshape(512, 256).T)) for x in wk_])
    m["wvc"] = np.stack([np.ascontiguousarray(np.asarray(x[b], f).reshape(4, 128, 256).transpose(1, 0, 2)) for x in wv_])
    return m


_NC = None


def kernel(**inputs):
    global _NC
    if _NC is None:
        _NC = build()
    sh = _prep_shared(inputs)
    in_maps = []
    for c in range(8):
        m = dict(sh)
        m.update(_prep_core(inputs, c))
        in_maps.append(m)
    res = run_bass_kernel_spmd(_NC, in_maps, core_ids=list(range(8)))
    R = res.results
    f = np.float32
    y_p = np.zeros((32, 256, 1024), f)
    y_s = np.zeros((4, 4096, 1024), f)
    outs = {k: [] for k in ("ckv", "krope", "rf", "rb", "wk", "wv")}
    ckv = [np.zeros((32, 256, 256), f) for _ in range(2)]
    krope = [np.zeros((32, 256, 32), f) for _ in range(2)]
    rf = [np.zeros((32, 4, 64, 128), f) for _ in range(2)]
    rb = [np.zeros((32, 4, 64, 128), f) for _ in range(2)]
    wk = [np.zeros((32, 256, 4, 64), f) for _ in range(2)]
    wv = [np.zeros((32, 256, 4, 64), f) for _ in range(2)]
    for c in range(8):
        r = R[c]
        yT = np.asarray(r["yT"])
        y_p[4 * c:4 * c + 4] = yT[:, 0:1024].T.reshape(4, 256, 1024)
        b, half = c // 2, c % 2
        y_s[b, half * 2048:(half + 1) * 2048] = yT[:, 1024 + half * 2048:1024 + (half + 1) * 2048].T
        for i in range(2):
            ckv[i][4 * c:4 * c + 4] = np.asarray(r[f"ckvT{i}"]).T.reshape(4, 256, 256)
            krope[i][4 * c:4 * c + 4] = np.asarray(r[f"kropeT{i}"]).T.reshape(4, 256, 32)
            rf[i][4 * c:4 * c + 4] = np.asarray(r[f"retf{i}"])
            rb[i][4 * c:4 * c + 4] = np.asarray(r[f"retb{i}"])
            wk[i][4 * c:4 * c + 4] = np.asarray(r[f"wkT{i}"]).T.reshape(4, 256, 4, 64)
            wv[i][4 * c:4 * c + 4] = np.asarray(r[f"wvo{i}"]).reshape(4, 256, 4, 64)
    return (y_p, y_s, ckv[0], krope[0], rf[0], rb[0], wk[0], wv[0],
            ckv[1], krope[1], rf[1], rb[1], wk[1], wv[1])
```

```python
import contextlib
import os
import numpy as np
import ml_dtypes
import concourse.bass as bass
import concourse.mybir as mybir
from concourse.bass_utils import run_bass_kernel_spmd

F32 = mybir.dt.float32
BF16 = mybir.dt.bfloat16
ALU = mybir.AluOpType
AF = mybir.ActivationFunctionType
NPBF = ml_dtypes.bfloat16

EPOCH = 24000
EPS = 1e-6
SHIFT = 10.0
NT = 5120
NK = 5632
NG = 10
DEPTH = 4
NLAYERS = int(os.environ.get('K_NLAYERS', '4'))
KSTOP = int(os.environ.get('K_STOP', '99'))
KSUB = int(os.environ.get('K_SUB', '99'))


class Buf:
    __slots__ = ("name", "w", "r", "excl")

    def __init__(self, name=""):
        self.name = name
        self.w = None
        self.r = {}
        self.excl = False


class Op:
    __slots__ = ("eng", "fn", "deps", "dma", "needs_inc", "tok")


class Sched:
    ENGS = ("pe", "act", "dve", "pool", "sp")
    NSLOT = 24

    def __init__(self):
        self.streams = {e: [] for e in self.ENGS}
        self.slot_last = {}
        self.slot_cnt = {}
        self.cls_n = {}

    def op(self, eng, fn, reads=(), writes=(), dma=None):
        o = Op()
        o.eng, o.fn, o.dma = eng, fn, dma
        o.needs_inc = dma is not None
        o.tok = None
        deps, seen = [], set()

        def add(p):
            if p is None or id(p) in seen:
                return
            if eng == "pe" and dma is None and p.eng == "pe" and p.dma is None:
                return
            seen.add(id(p))
            deps.append(p)

        for b in reads:
            add(b.w)
            if b.excl:
                for kk, p in b.r.items():
                    if kk != ("e", eng):
                        add(p)
        for b in writes:
            add(b.w)
            for p in b.r.values():
                add(p)
        if dma is not None:
            n = self.cls_n.get(dma, 0)
            self.cls_n[dma] = n + 1
            slot = (dma, n % self.NSLOT)
            add(self.slot_last.get(slot))
            c = self.slot_cnt.get(slot, 0) + 1
            self.slot_cnt[slot] = c
            self.slot_last[slot] = o
            o.tok = (("d",) + slot, 16 * c)
            okey = ("d",) + slot
        else:
            okey = ("e", eng)
        o.deps = deps
        for p in deps:
            p.needs_inc = True
        for b in writes:
            b.w = o
            b.r = {}
        for b in reads:
            if b.w is not o:
                b.r[okey] = o
        self.streams[eng].append(o)
        return o

    def frontier(self):
        fr = {}
        for e, s in self.streams.items():
            for o in reversed(s):
                if o.dma is None:
                    fr[("e", e)] = o
                    break
        for slot, o in self.slot_last.items():
            fr[("d",) + slot] = o
        return fr

    def store_finals(self, cls="st"):
        return [o for slot, o in self.slot_last.items() if slot[0] == cls]

    def emit(self, nc, finals):
        for e in self.ENGS:
            cnt = 0
            for o in self.streams[e]:
                if o.dma is None and o.needs_inc:
                    ep = cnt // EPOCH
                    cnt += 1
                    o.tok = (("e", e, ep), cnt - ep * EPOCH)
        keys = []
        for e in self.ENGS:
            for o in self.streams[e]:
                if o.needs_inc and o.tok[0] not in keys:
                    keys.append(o.tok[0])
        with contextlib.ExitStack() as st:
            sems = {k: st.enter_context(nc.semaphore("s_" + "_".join(str(x) for x in k))) for k in keys}
            block = st.enter_context(nc.Block())
            engobj = {"pe": block.tensor, "act": block.scalar, "dve": block.vector,
                      "pool": block.gpsimd, "sp": block.sync}

            def make(e):
                ops = self.streams[e]

                def body(eng):
                    waited = {}

                    def wait_for(plist):
                        need = {}
                        for p in plist:
                            k, v = p.tok
                            if waited.get(k, 0) >= v:
                                continue
                            if need.get(k, 0) < v:
                                need[k] = v
                        for k, v in need.items():
                            eng.wait_ge(sems[k], v)
                            waited[k] = v

                    for o in ops:
                        wait_for(o.deps)
                        ins = o.fn(eng)
                        if o.needs_inc:
                            ins.then_inc(sems[o.tok[0]], 16 if o.dma is not None else 1)
                    if e == "sp":
                        wait_for(finals)
                return body

            for e in self.ENGS:
                if self.streams[e] or e == "sp":
                    engobj[e](make(e))


class T:
    def __init__(self, ap, name=""):
        self.ap = ap
        self.b = Buf(name)

    def __getitem__(self, idx):
        return self.ap[idx]


class LV:
    def __init__(self, t):
        self.t = t
        self.b = t.b
        self.cur = 0

    def __getitem__(self, idx):
        if not isinstance(idx, tuple):
            idx = (idx,)
        return self.t[(idx[0], self.cur) + tuple(idx[1:])]


class Ring:
    def __init__(self, items):
        self.items = items
        self.i = 0

    def next(self):
        t = self.items[self.i % len(self.items)]
        self.i += 1
        return t


def _rope_tables(half_dims):
    pass


def _build_consts():
    C = {}
    t = np.arange(4096)
    rows = (t // 64).astype(np.float32)
    cols = (t % 64).astype(np.float32)

    def rope_block(half, pos):
        inv = np.power(np.float32(10000.0), -np.arange(half, dtype=np.float32) / np.float32(half)).astype(np.float32)
        ang = (pos[None, :] * inv[:, None]).astype(np.float32)
        c, s = np.cos(ang).astype(np.float32), np.sin(ang).astype(np.float32)
        cosb = np.concatenate([c, c], 0)
        sinb = np.concatenate([-s, s], 0)
        return cosb, sinb

    c1, s1 = rope_block(8, rows)
    c2, s2 = rope_block(8, cols)
    cosA = np.concatenate([np.ones((64, 4096), np.float32), c1, c2], 0)
    sinA = np.concatenate([np.zeros((64, 4096), np.float32), s1, s2], 0)
    c1, s1 = rope_block(16, rows)
    c2, s2 = rope_block(16, cols)
    cb = np.concatenate([c1, c2], 0)
    sb = np.concatenate([s1, s2], 0)
    cosB = np.concatenate([cb, cb], 0)
    sinB = np.concatenate([sb, sb], 0)
    C["ropeA"] = np.ascontiguousarray(np.stack([cosA, sinA]))
    C["ropeB"] = np.ascontiguousarray(np.stack([cosB, sinB]))

    def perm(n, blocks):
        P = np.zeros((128, 128), np.float32)
        for (base, half) in blocks:
            for j in range(half):
                P[base + half + j, base + j] = 1.0
                P[base + j, base + half + j] = 1.0
        return P

    permA = perm(96, [(64, 8), (80, 8)])
    permB = perm(128, [(0, 16), (32, 16), (64, 16), (96, 16)])
    ident = np.eye(128, dtype=np.float32)
    ones = np.ones((128, 128), np.float32)
    ones64 = np.zeros((128, 128), np.float32)
    ones64[0:64, 0:64] = 1
    ones64[64:128, 64:128] = 1
    wk3 = np.zeros((128, 128), np.float32)
    for j in range(32):
        wk3[j, 64 + j] = 1.0
    k = np.arange(128)[:, None]
    q = np.arange(128)[None, :]
    mprev = np.tile((k >= q).astype(np.float32), (1, 4))
    mnext = np.tile((k <= q).astype(np.float32), (1, 4))
    cbf = np.concatenate([permA, permB, ident, ones, ones64, wk3, mprev, mnext], 1)
    C["cbf"] = cbf.astype(NPBF)
    RPf = np.maximum(q - k, 0).astype(np.float32)
    RPb = np.maximum(k - q, 0).astype(np.float32)
    INDf = (q >= k).astype(np.float32)
    INDb = (k >= q).astype(np.float32)
    QP1 = np.broadcast_to((q + 1).astype(np.float32), (128, 128))
    CMQ = np.broadcast_to((128 - q).astype(np.float32), (128, 128))
    colsm = np.zeros((128, 128), np.float32)
    colsm[:, 0] = 127 - np.arange(128)
    colsm[:, 1] = np.arange(128)
    colsm[:, 2] = EPS
    colsm[:, 3] = -SHIFT
    colsm[:, 4] = 1.0
    e64 = np.zeros((128, 128), np.float32)
    e64[:, 64] = 1.0
    C["cf32"] = np.ascontiguousarray(np.concatenate([RPf, RPb, INDf, INDb, QP1, CMQ, colsm, e64, ones], 1))
    return C


_CONSTS = None

CB_PERMA, CB_PERMB, CB_IDENT, CB_ONES, CB_ONES64, CB_WK3, CB_MPREV, CB_MNEXT = 0, 128, 256, 384, 512, 640, 768, 1280
CB_N = 1792
CF_RPF, CF_RPB, CF_INDF, CF_INDB, CF_QP1, CF_CMQ, CF_COLS, CF_E64, CF_ONES = [128 * i for i in range(9)]
CF_N = 128 * 9
PE_QNG, PE_KVNG, PE_QHG, PE_KHG, PE_RNG, PE_NORMG, PE_ADAB = 0, 3, 5, 6, 7, 11, 19
PE_N = 43
PO_Q64, PO_K64, PO_NORMG, PO_ADAB = 0, 1, 2, 10
PO_N = 34


class KB:
    def __init__(self):
        self.nc = bass.Bass("TRN2", target_bir_lowering=False)
        self.S = Sched()
        self.st = contextlib.ExitStack()
        self.finals = []
        self.hooks = {}

    def dram_in(self, name, shape, dt=F32):
        return self.nc.dram_tensor(name, list(shape), dt, kind="ExternalInput").ap()

    def dram_out(self, name, shape, dt=F32):
        return self.nc.dram_tensor(name, list(shape), dt, kind="ExternalOutput").ap()

    def dram_scr(self, name, shape, dt):
        return T(self.nc.dram_tensor(name, list(shape), dt).ap(), name)

    def sb(self, name, shape, dt):
        return T(self.st.enter_context(self.nc.sbuf_tensor("sb_" + name, list(shape), dt)), name)

    def phase(self, tag=None):
        self.aoff = 0
        self.front = self.S.frontier()
        if tag is not None:
            for f in self.hooks.pop(tag, []):
                f()

    def carve(self, cols, dt, name="", shape3=None):
        n32 = (cols * (2 if dt == BF16 else 4) + 3) // 4
        n32 = (n32 + 7) // 8 * 8
        assert self.aoff + n32 <= self.acols, ("arena overflow", name, self.aoff, n32, self.acols)
        ap = self.arena[:, self.aoff:self.aoff + n32]
        self.aoff += n32
        if dt == BF16:
            ap = ap.bitcast(BF16)
        ap = ap[:, 0:cols]
        if shape3 is not None:
            ap = ap.rearrange("p (c n) -> p c n", c=shape3)
        t = T(ap, name)
        t.b.r = dict(self.front)
        return t

    def ring(self, n, cols, dt, name, shape3=None):
        return Ring([self.carve(cols, dt, f"{name}{i}", shape3) for i in range(n)])

    def mm(self, out, lhsT, rhs, start, stop, reads, writes):
        self.S.op("pe", lambda e: e.matmul(out, lhsT=lhsT, rhs=rhs, start=start, stop=stop),
                  reads=[t.b for t in reads], writes=[t.b for t in writes])

    def transpose(self, out, in_, ident, reads, writes):
        self.S.op("pe", lambda e: e.transpose(out, in_, ident),
                  reads=[t.b for t in reads], writes=[t.b for t in writes])

    def act(self, out, in_, func, reads, writes, bias=None, scale=None):
        kw = {}
        if bias is not None:
            kw["bias"] = bias
        if scale is not None:
            kw["scale"] = scale
        self.S.op("act", lambda e: e.activation(out=out, in_=in_, func=func, **kw),
                  reads=[t.b for t in reads], writes=[t.b for t in writes])

    def stt(self, out, in0, scalar, in1, op0, op1, reads, writes, eng="dve"):
        self.S.op(eng, lambda e: e.scalar_tensor_tensor(out=out, in0=in0, scalar=scalar, in1=in1, op0=op0, op1=op1),
                  reads=[t.b for t in reads], writes=[t.b for t in writes])

    def tt(self, out, in0, in1, op, reads, writes, eng="dve"):
        self.S.op(eng, lambda e: e.tensor_tensor(out=out, in0=in0, in1=in1, op=op),
                  reads=[t.b for t in reads], writes=[t.b for t in writes])

    def ts(self, out, in0, s1, s2, op0, op1, reads, writes, eng="dve"):
        if s2 is None:
            self.S.op(eng, lambda e: e.tensor_scalar(out=out, in0=in0, scalar1=s1, scalar2=None, op0=op0),
                      reads=[t.b for t in reads], writes=[t.b for t in writes])
        else:
            self.S.op(eng, lambda e: e.tensor_scalar(out=out, in0=in0, scalar1=s1, scalar2=s2, op0=op0, op1=op1),
                      reads=[t.b for t in reads], writes=[t.b for t in writes])

    def copy(self, out, in_, reads, writes, eng="dve"):
        if eng == "act":
            self.S.op("act", lambda e: e.copy(out=out, in_=in_), reads=[t.b for t in reads], writes=[t.b for t in writes])
        else:
            self.S.op(eng, lambda e: e.tensor_copy(out=out, in_=in_), reads=[t.b for t in reads], writes=[t.b for t in writes])

    def recip(self, out, in_, reads, writes):
        def f(e):
            with self.nc.allow_low_precision(reason="bf16 reciprocal row feeds 1-pass broadcast matmul"):
                return e.reciprocal(out=out, in_=in_)
        self.S.op("dve", f, reads=[t.b for t in reads], writes=[t.b for t in writes])

    def memset(self, out, val, writes, eng="pool"):
        self.S.op(eng, lambda e: e.memset(out, val), writes=[t.b for t in writes])

    def load(self, out, in_, reads, writes):
        self.S.op("sp", lambda e: e.dma_start(out=out, in_=in_), reads=[t.b for t in reads],
                  writes=[t.b for t in writes], dma="ld")

    def load2(self, out, in_, reads, writes):
        self.S.op("pool", lambda e: e.dma_start(out=out, in_=in_), reads=[t.b for t in reads],
                  writes=[t.b for t in writes], dma="ld2")

    def store(self, out, in_, reads, writes, final=False):
        o = self.S.op("pool", lambda e: e.dma_start(out=out, in_=in_), reads=[t.b for t in reads],
                      writes=[t.b for t in writes], dma="st")
        return o

    def bank(self):
        return self.banks.next()

    def dbg(self, name, ap, t, dt=F32):
        if os.environ.get("K_DEBUG") != "1":
            return
        shp = list(ap.shape)
        d = self.nc.dram_tensor("dbg_" + name, shp, dt).ap()
        self.S.op("pool", lambda e: e.dma_start(out=d, in_=ap), reads=[t.b], dma="st")


def build():
    kb = KB()
    nc = kb.nc
    D = {}
    D["xin"] = kb.dram_in("xin", [1024, NT])
    D["cond"] = kb.dram_in("cond", [128, 16])
    D["ada_w"] = kb.dram_in("ada_w", [4, 128, 8, 3072])
    D["w_in_e"] = kb.dram_in("w_in_e", [2, 128, 8, 2720])
    D["w_in_o"] = kb.dram_in("w_in_o", [2, 128, 8, 2560])
    D["w_out"] = kb.dram_in("w_out", [4, 128, 8, 1024])
    D["w_uq"] = kb.dram_in("w_uq", [2, 128, 3, 768])
    D["wk"] = kb.dram_in("wk", [2, 128, 2, 768])
    D["wv"] = kb.dram_in("wv", [2, 128, 2, 512])
    D["pc_all"] = kb.dram_in("pc_all", [128, 4, 48])
    D["sinkrow"] = kb.dram_in("sinkrow", [2, 1, 2048])
    D["decay"] = kb.dram_in("decay", [2, 1, 8])
    D["ckvc"] = kb.dram_in("ckvc", [2, 128, 2, 512])
    D["kropec"] = kb.dram_in("kropec", [2, 32, 512])
    D["st_f"] = kb.dram_in("st_f", [2, 64, 4, 128])
    D["st_b"] = kb.dram_in("st_b", [2, 64, 4, 128])
    D["wkc"] = kb.dram_in("wkc", [2, 128, 2, 512])
    D["wvc"] = kb.dram_in("wvc", [2, 128, 4, 256])
    D["ropeA"] = kb.dram_in("ropeA", [2, 96, 4096])
    D["ropeB"] = kb.dram_in("ropeB", [2, 128, 4096])
    D["cbf"] = kb.dram_in("cbf", [128, CB_N], BF16)
    D["cf32"] = kb.dram_in("cf32", [128, CF_N])
    O = {}
    O["yT"] = kb.dram_out("yT", [1024, NT])
    for i in range(2):
        O[f"ckvT{i}"] = kb.dram_out(f"ckvT{i}", [256, 1024])
        O[f"kropeT{i}"] = kb.dram_out(f"kropeT{i}", [32, 1024])
        O[f"retf{i}"] = kb.dram_out(f"retf{i}", [4, 4, 64, 128])
        O[f"retb{i}"] = kb.dram_out(f"retb{i}", [4, 4, 64, 128])
        O[f"wkT{i}"] = kb.dram_out(f"wkT{i}", [256, 1024])
        O[f"wvo{i}"] = kb.dram_out(f"wvo{i}", [1024, 256])
    OUTT = T(None, "outs")
    XT = kb.dram_scr("XT", [1024, NT], F32)
    QT = kb.dram_scr("QT", [8, 128, NT], BF16)
    KT = kb.dram_scr("KT", [8, 128, NK], BF16)
    VA = kb.dram_scr("VA", [NK, 512], BF16)
    GT = kb.dram_scr("GT", [8, 128, NT], BF16)
    MT = kb.dram_scr("MT", [8, 128, NT], BF16)
    RQT = kb.dram_scr("RQT", [2, 128, NT], BF16)
    RKT = kb.dram_scr("RKT", [2, 128, NT], BF16)
    RV = kb.dram_scr("RV", [NT, 512], BF16)
    XIN = T(D["xin"], "xin")

    st = kb.st
    with st:
        cbf = kb.sb("cbf", [128, CB_N], BF16)
        cf = kb.sb("cf", [128, CF_N], F32)
        WIN = kb.sb("WIN", [128, 8, 2720], BF16)
        WOUT = kb.sb("WOUT", [128, 8, 1024], BF16)
        WUQ = kb.sb("WUQ", [128, 3, 768], BF16)
        WK = kb.sb("WK", [128, 2, 768], BF16)
        WV = kb.sb("WV", [128, 2, 512], BF16)
        stage = Ring([kb.sb(f"stage{i}", [128, 1360], F32) for i in range(2)])
        pc = LV(kb.sb("pc", [128, 4, 48], F32))
        modT = LV(kb.sb("modT", [128, 4, 24, 2], F32))
        gAT = LV(kb.sb("gAT", [128, 4, 8, 2], F32))

        def set_layer(l):
            pc.cur = modT.cur = gAT.cur = l
        condS = kb.sb("condS", [128, 16], F32)
        lg = kb.sb("lg", [128, 8], F32)
        rtab = kb.sb("rtab", [128, 4, 128], F32)
        qtab = kb.sb("qtab", [64, 8, 128], F32)
        rcols = kb.sb("rcols", [128, 24], F32)
        sinkB = kb.sb("sinkB", [1, 2048], BF16)
        e64b = kb.sb("e64b", [1, 128], BF16)
        kb.acols = (nc.SBUF_PARTITION_SIZE_BYTES - 1024) // 4
        used = (CB_N * 2 + CF_N * 4 + 2 * (8 * 2720 + 8 * 1024 + 3 * 768 + 2 * 768 + 2 * 512) + 2 * 1360 * 4
                + PE_N * 4 + 48 * 4 + 16 * 4 + 16 * 4 + 8 * 4 + 512 * 4 + 8 * 128 * 4 + 24 * 4 + 2048 * 4)
        kb.acols = 27400
        kb.arena = st.enter_context(nc.sbuf_tensor("arena", [128, kb.acols], F32))
        ppair_t = [st.enter_context(nc.psum_tensor(f"pp{i}", [128, 1024], F32)) for i in range(4)]
        pbanks = [T(ppair_t[i // 2][:, (i % 2) * 512:(i % 2 + 1) * 512], f"pb{i}") for i in range(8)]
        stp = Ring([(ppair_t[k][:, :], [pbanks[2 * k], pbanks[2 * k + 1]]) for k in range(3)])
        miscb = Ring(pbanks[4:6])
        for pbk in pbanks:
            pbk.b.excl = True
        kb.banks = Ring(pbanks[0:6])
        accb = Ring(pbanks[6:8])

        eps_c = cf[:, CF_COLS + 2:CF_COLS + 3]
        nshift_c = cf[:, CF_COLS + 3:CF_COLS + 4]
        ones_bf = cbf[:, CB_ONES:CB_ONES + 128]
        ones64_bf = cbf[:, CB_ONES64:CB_ONES64 + 128]

        kb.load(cbf[:, :], D["cbf"][:, :], [], [cbf])
        kb.load(cf[:, :], D["cf32"][:, :], [], [cf])
        kb.load(condS[:, :], D["cond"][:, :], [], [condS])
        kb.act(condS[:, :], condS[:, :], AF.Silu, [condS], [condS])
        kb.copy(e64b[0:1, :], cf[0:1, CF_E64:CF_E64 + 128], [cf], [e64b])

        def load_weight(dst, dst_view2d, src2d, ncols):
            c0 = 0
            while c0 < ncols:
                n = min(1360, ncols - c0)
                sg = stage.next()
                kb.load2(sg[:, 0:n], src2d[:, c0:c0 + n], [], [sg])
                kb.copy(dst_view2d[:, c0:c0 + n], sg[:, 0:n], [sg], [dst], eng="pool")
                c0 += n

        def flat(t3):
            return t3[:, :, :].rearrange("p a b -> p (a b)")

        def rstd_from(ps, rows, Dn, dst, rd):
            kb.act(dst[0:rows, :], ps[0:rows, :], AF.Ln, [ps, cf] + rd, [dst], bias=eps_c[0:rows, :], scale=1.0 / Dn)
            kb.act(dst[0:rows, :], dst[0:rows, :], AF.Exp, [dst], [dst], scale=-0.5)

        def x_norm_h(xsrc, g, j, Lr):
            xg, hT, rstd = Lr["xg"], Lr["hT"], Lr["rstd"]
            kb.load(xg[:, :, :], xsrc.ap.rearrange("(c p) t -> p c t", p=128)[:, :, g * 512:(g + 1) * 512], [xsrc], [xg])
            ps = kb.bank()
            for c in range(8):
                sq = Lr["rb"].next()
                kb.act(sq[:, :], xg[:, c, :], AF.Square, [xg], [sq])
                kb.mm(ps[:, :], ones_bf, sq[:, :], c == 0, c == 7, [sq, cbf], [ps])
            rstd_from(ps, 128, 1024.0, rstd, [])
            for c in range(8):
                tf = Lr["rf"].next()
                kb.stt(tf[:, :], xg[:, c, :], gAT[:, c, j:j + 1], rstd[:, :], ALU.mult, ALU.mult, [xg, gAT, rstd], [tf])
                kb.act(hT[:, c, :], tf[:, :], AF.Identity, [tf, modT], [hT], bias=modT[:, c, j:j + 1], scale=1.0)
            return xg, hT

        def rope_apply(xf, rows, perm_ap, cosT, sinT, dst_ap, dst_t, Lr):
            xb = Lr["rb"].next()
            kb.copy(xb[0:rows, :], xf[0:rows, :], [xf], [xb], eng="act")
            ps = kb.bank()
            kb.mm(ps[0:rows, :], perm_ap[0:rows, 0:rows], xb[0:rows, :], True, True, [xb, cbf], [ps])
            t1 = Lr["rf"].next()
            kb.tt(t1[0:rows, :], xf[0:rows, :], cosT[0:rows, :], ALU.mult, [xf, cosT], [t1])
            t2 = Lr["rf"].next()
            kb.tt(t2[0:rows, :], ps[0:rows, :], sinT[0:rows, :], ALU.mult, [ps, sinT], [t2])
            kb.tt(dst_ap, t1[0:rows, :], t2[0:rows, :], ALU.add, [t1, t2], [dst_t])

        def head_norm_rope(ps, rows, Dn, ones_ap, gcol, rope, dst_ap, dst_t, Lr, prescale=None, f32_out=None):
            raw = Lr["rf"].next()
            kb.copy(raw[0:rows, :], ps[0:rows, :], [ps], [raw], eng="act")
            sq = Lr["rb"].next()
            kb.act(sq[0:rows, :], ps[0:rows, :], AF.Square, [ps], [sq])
            ps2 = kb.bank()
            kb.mm(ps2[0:rows, :], ones_ap[0:rows, 0:rows], sq[0:rows, :], True, True, [sq, cbf], [ps2])
            rs = Lr["rf"].next()
            rstd_from(ps2, rows, Dn, rs, [])
            if rope is None and f32_out is None:
                kb.stt(dst_ap, raw[0:rows, :], gcol, rs[0:rows, :], ALU.mult, ALU.mult, [raw, rs, pc], [dst_t])
                return
            xf = Lr["rf"].next() if f32_out is None else f32_out
            kb.stt(xf[0:rows, :], raw[0:rows, :], gcol, rs[0:rows, :], ALU.mult, ALU.mult, [raw, rs, pc], [xf])
            if rope is None:
                kb.copy(dst_ap, xf[0:rows, :], [xf], [dst_t], eng="act")
            else:
                rope_apply(xf, rows, rope[0], rope[1], rope[2], dst_ap, dst_t, Lr)

        def make_rings():
            R = {}
            for nm in ("raw", "rs", "xf", "t1", "t2"):
                R[nm] = kb.ring(3, 512, F32, nm)
            for nm in ("sq", "xb"):
                R[nm] = kb.ring(3, 512, BF16, nm)
            R["ko"] = kb.ring(4, 512, BF16, "ko")
            return R

        def run_pipeline(jobs, offs=(0, 2, 4)):
            ns = max(len(jb) for jb in jobs)
            for t in range(len(jobs) + offs[ns - 1]):
                for sidx in range(ns):
                    jj = t - offs[sidx]
                    if 0 <= jj < len(jobs) and sidx < len(jobs[jj]) and jobs[jj][sidx] is not None:
                        jobs[jj][sidx]()

        def rope_tail_stage(xf, xb, rows, rope, dst_ap, dst_t, R):
            perm_ap, cosT, sinT = rope
            ps = kb.bank()
            kb.mm(ps[0:rows, :], perm_ap[0:rows, 0:rows], xb[0:rows, :], True, True, [xb, cbf], [ps])
            t1 = R["t1"].next()
            kb.tt(t1[0:rows, :], xf[0:rows, :], cosT[0:rows, :], ALU.mult, [xf, cosT], [t1])
            t2 = R["t2"].next()
            kb.tt(t2[0:rows, :], ps[0:rows, :], sinT[0:rows, :], ALU.mult, [ps, sinT], [t2])
            kb.tt(dst_ap, t1[0:rows, :], t2[0:rows, :], ALU.add, [t1, t2], [dst_t])

        def hnr_job(mm_fn, rows, Dn, ones_ap, gcol, rope, R, post, f32_out_fn=None):
            stt_ = {}

            def A():
                ps = kb.bank()
                mm_fn(ps)
                raw = R["raw"].next()
                kb.copy(raw[0:rows, :], ps[0:rows, :], [ps], [raw], eng="act")
                sq = R["sq"].next()
                kb.act(sq[0:rows, :], ps[0:rows, :], AF.Square, [ps], [sq])
                stt_["raw"], stt_["sq"] = raw, sq

            def B():
                raw, sq = stt_["raw"], stt_["sq"]
                ps2 = kb.bank()
                kb.mm(ps2[0:rows, :], ones_ap[0:rows, 0:rows], sq[0:rows, :], True, True, [sq, cbf], [ps2])
                rs = R["rs"].next()
                rstd_from(ps2, rows, Dn, rs, [])
                ko = R["ko"].next()
                stt_["ko"] = ko
                f32_out = f32_out_fn() if f32_out_fn is not None else None
                if rope is None and f32_out is None:
                    kb.stt(ko[0:rows, :], raw[0:rows, :], gcol, rs[0:rows, :], ALU.mult, ALU.mult, [raw, rs, pc], [ko])
                    post(ko)
                    return
                xf = R["xf"].next() if f32_out is None else f32_out
                kb.stt(xf[0:rows, :], raw[0:rows, :], gcol, rs[0:rows, :], ALU.mult, ALU.mult, [raw, rs, pc], [xf])
                stt_["xf"] = xf
                if rope is None:
                    kb.copy(ko[0:rows, :], xf[0:rows, :], [xf], [ko], eng="act")
                    post(ko)
                else:
                    xb = R["xb"].next()
                    kb.copy(xb[0:rows, :], xf[0:rows, :], [xf], [xb])
                    stt_["xb"] = xb

            def C():
                if rope is None:
                    return
                ko = stt_["ko"]
                rope_tail_stage(stt_["xf"], stt_["xb"], rows, rope, ko[0:rows, :], ko, R)
                post(ko)
            return [A, B, C]

        def inproj(hT, c0, m, ps):
            for c in range(8):
                kb.mm(ps[0:m, :], WIN[:, c, c0:c0 + m], hT[:, c, :], c == 0, c == 7, [WIN, hT], [ps])

        def kbase(g):
            return g * 512 if g < 2 else 1536 + (g - 2) * 512

        class AStream:
            def __init__(self):
                self.blocks = []

            def add(self, q_fn, segs, acc, extra, post):
                self.blocks.append({"q": q_fn, "segs": segs, "acc": acc, "extra": extra, "post": post, "pre": []})
                return len(self.blocks) - 1

            def attach(self, idx, fn):
                self.blocks[max(idx, 0)]["pre"].append(fn)

            def run(self, ppairs, Lr, LA=2, DEFER=int(os.environ.get("K_DEFER", "3"))):
                units = []
                for b in self.blocks:
                    first_u = True
                    nseg = len(b["segs"])
                    for si, (c0, ncol, ktiles, rd, d, scale) in enumerate(b["segs"]):
                        nk = len(ktiles)
                        i = 0
                        while i < nk:
                            n = 2 if i + 1 < nk else 1
                            lastu = (si == nseg - 1) and (i + n == nk)
                            units.append({"b": b, "c0": c0, "ncol": ncol, "tiles": ktiles[i:i + n], "rd": rd, "scale": scale,
                                          "first": i == 0, "last": i + n == nk,
                                          "pre": b["pre"] if first_u else [], "post": b["post"] if lastu else None})
                            first_u = False
                            i += n

                def issue_s(u):
                    b, c0, ncol, tiles, rd, scale = u["b"], u["c0"], u["ncol"], u["tiles"], u["rd"], u["scale"]
                    pr = stp.next()
                    pT = ppairs.next()
                    qr = b["q"](c0, ncol)
                    n = len(tiles)
                    for j, (kT_ap, va_ap, mask_ap) in enumerate(tiles):
                        bk = pr[1][j]
                        pso = bk[:, 0:ncol]
                        if len(qr.shape) == 3:
                            pso = pso.rearrange("p (h q) -> p h q", h=qr.shape[1])
                        kb.mm(pso, kT_ap, qr, True, True, rd, [bk])
                    pv = pr[0].rearrange("p (b n) -> p b n", b=2)
                    tv = pT[:, :].rearrange("p (b n) -> p b n", b=2)
                    kb.act(tv[:, 0:n, 0:ncol], pv[:, 0:n, 0:ncol], AF.Exp, [pr[1][j] for j in range(n)] + [cf], [pT],
                           bias=nshift_c, scale=scale)
                    for j, (kT_ap, va_ap, mask_ap) in enumerate(tiles):
                        if mask_ap is not None:
                            kb.tt(pT[:, j * 512:j * 512 + ncol], pT[:, j * 512:j * 512 + ncol], mask_ap, ALU.mult, [pT, cbf], [pT])
                    return pT

                def issue_pv(u, pT):
                    b, c0, ncol, tiles, rd = u["b"], u["c0"], u["ncol"], u["tiles"], u["rd"]
                    acc, extra = b["acc"], b["extra"]
                    n = len(tiles)
                    for j, (kT_ap, va_ap, mask_ap) in enumerate(tiles):
                        st_ = u["first"] and j == 0
                        sp_ = u["last"] and j == n - 1 and extra is None
                        kb.mm(acc[0:65, c0:c0 + ncol], va_ap, pT[:, j * 512:j * 512 + ncol], st_, sp_, [pT] + rd, [acc])
                    if u["last"] and extra is not None:
                        lhsT, rhs, rdx = extra
                        kb.mm(acc[0:65, c0:c0 + ncol], lhsT, rhs, False, True, rdx, [acc])

                pend = []
                deferred = []
                nU = len(units)
                for i in range(nU + LA):
                    if i < nU:
                        u = units[i]
                        for f in u["pre"]:
                            f()
                        pend.append((u, issue_s(u)))
                    keep = []
                    for (cnt, f) in deferred:
                        if cnt <= 1:
                            f()
                        else:
                            keep.append((cnt - 1, f))
                    deferred = keep
                    if i >= LA:
                        u, pT = pend.pop(0)
                        issue_pv(u, pT)
                        if u["post"] is not None:
                            b = u["b"]
                            rr, osb = attn_finish_a(b["acc"], Lr)
                            deferred.append((DEFER, lambda rr=rr, osb=osb, post=u["post"]: post(rr, osb)))
                for (cnt, f) in deferred:
                    f()

        def attn_finish_a(acc, Lr):
            rr = Lr["rrow"].next()
            if Lr.get("act_recip"):
                ln_ = Lr["rf"].next()
                kb.act(ln_[64:65, :], acc[64:65, :], AF.Ln, [acc], [ln_])
                kb.act(rr[64:65, :], ln_[64:65, :], AF.Exp, [ln_], [rr], scale=-1.0)
            else:
                kb.recip(rr[64:65, :], acc[64:65, :], [acc], [rr])
            osb = Lr["rf"].next()
            kb.copy(osb[0:64, :], acc[0:64, :], [acc], [osb])
            return rr, osb

        def attn_finish_b(rr, osb, gt_ap, gt_t, dst_ap, dst_t, Lr, after=None):
            ps = stp.next()[1][0]
            kb.mm(ps[0:64, :], cbf[64:65, CB_ONES:CB_ONES + 64], rr[64:65, :], True, True, [rr, cbf], [ps])
            a = Lr["rf"].next()
            kb.tt(a[0:64, :], osb[0:64, :], ps[0:64, :], ALU.mult, [osb, ps], [a])
            a_ap = a[0:64, :]
            if len(dst_ap.shape) == 3:
                a_ap = a_ap.rearrange("p (h q) -> p h q", h=dst_ap.shape[1])
            kb.tt(dst_ap, a_ap, gt_ap, ALU.mult, [a, gt_t], [dst_t])
            if after is not None:
                after()

        def phase_out(l, xsrc, xdst_ap, xdst_t):
            kb.phase()
            Lr = {}
            xgs = kb.ring(2, 8 * 512, F32, "xg", 8)
            mss = kb.ring(2, 8 * 512, BF16, "ms", 8)
            outs = kb.ring(3, 512, F32, "xo")
            for g in range(NG):
                j = 0 if g < 2 else 1
                xg, ms = xgs.next(), mss.next()
                kb.load(xg[:, :, :], xsrc.ap.rearrange("(c p) t -> p c t", p=128)[:, :, g * 512:(g + 1) * 512], [xsrc], [xg])
                kb.load(ms[:, :, :], MT.ap.rearrange("c p t -> p c t")[:, :, g * 512:(g + 1) * 512], [MT], [ms])
                for m in range(8):
                    ps = kb.bank()
                    for c in range(8):
                        kb.mm(ps[:, :], WOUT[:, c, m * 128:(m + 1) * 128], ms[:, c, :], c == 0, c == 7, [WOUT, ms], [ps])
                    xo = outs.next()
                    kb.stt(xo[:, :], ps[:, :], modT[:, 16 + m, j:j + 1], xg[:, m, :], ALU.mult, ALU.add, [ps, modT, xg], [xo])
                    o = kb.store(xdst_ap[m * 128:(m + 1) * 128, g * 512:(g + 1) * 512], xo[:, :], [xo], [xdst_t])
                    if xdst_t is OUTT:
                        kb.finals.append(o)

        def phase_mod(l, pcn, normg_off, adab_off):
            kb.phase()
            ast = kb.ring(2, 8 * 512, F32, "adast", 8)
            mrow = kb.carve(3072, F32, "mrow")
            id2 = kb.carve(8, F32, "id2")
            kb.copy(id2[0:2, 0:2], cbf[0:2, CB_IDENT:CB_IDENT + 2], [cbf], [id2])
            for nck in range(6):
                sg = ast.next()
                kb.load(sg[:, :, :], D["ada_w"][l, :, :, nck * 512:(nck + 1) * 512], [], [sg])
                psr = kb.bank()
                for c in range(8):
                    kb.mm(psr[0:2, :], condS[:, 2 * c:2 * c + 2], sg[:, c, :], c == 0, c == 7, [sg, condS], [psr])
                kb.copy(mrow[0:2, nck * 512:(nck + 1) * 512], psr[0:2, :], [psr], [mrow], eng="act")
            ps = kb.bank()
            for m in range(24):
                kb.mm(ps[:, 2 * m:2 * m + 2], mrow[0:2, m * 128:(m + 1) * 128], id2[0:2, 0:2], True, True, [mrow, id2], [ps])
            psv = ps[:, 0:48].rearrange("p (m j) -> p m j", j=2)
            for j in range(2):
                kb.tt(modT[:, :, j], psv[:, :, j], pc[:, adab_off:adab_off + 24], ALU.add, [ps, pc], [modT])
                tmp = kb.carve(8, F32, "modtmp")
                kb.ts(tmp[:, :], modT[:, 8:16, j], 1.0, None, ALU.add, None, [modT], [tmp])
                kb.tt(gAT[:, :, j], tmp[:, :], pc[:, normg_off:normg_off + 8], ALU.mult, [tmp, pc], [gAT])

        def prefetch_in(l):
            if l >= NLAYERS:
                return
            i = l // 2
            if l % 2 == 0:
                load_weight(WIN, flat(WIN), D["w_in_e"][i].rearrange("p a b -> p (a b)"), 8 * 2720)
                load_weight(WUQ, flat(WUQ), D["w_uq"][i].rearrange("p a b -> p (a b)"), 3 * 768)
                load_weight(WK, flat(WK), D["wk"][i].rearrange("p a b -> p (a b)"), 2 * 768)
                load_weight(WV, flat(WV), D["wv"][i].rearrange("p a b -> p (a b)"), 2 * 512)
            else:
                load_weight(WIN, flat(WIN)[:, 0:8 * 2560], D["w_in_o"][i].rearrange("p a b -> p (a b)"), 8 * 2560)

        def prefetch_out(l):
            if l >= NLAYERS:
                return
            load_weight(WOUT, flat(WOUT), D["w_out"][l].rearrange("p a b -> p (a b)"), 8 * 1024)

        def even_layer(l, xsrc, xdst_ap, xdst_t):
            i = l // 2
            set_layer(l)
            if KSTOP <= 1:
                return
            kb.phase()
            kb.load(lg[:, :], D["decay"][i, 0:1, :].broadcast_to([128, 8]), [], [lg])
            kb.act(lg[:, :], lg[:, :], AF.Exp, [lg], [lg])
            kb.ts(lg[:, :], lg[:, :], -1.0, None, ALU.mult, None, [lg], [lg])
            tA = kb.carve(128, F32, "tA")
            tB = kb.carve(128, F32, "tB")
            for h in range(4):
                kb.act(tA[:, :], cf[:, CF_RPF:CF_RPF + 128], AF.Exp, [cf, lg], [tA], scale=lg[:, h:h + 1])
                kb.tt(tA[:, :], tA[:, :], cf[:, CF_INDF:CF_INDF + 128], ALU.mult, [tA, cf], [tA])
                kb.act(tB[:, :], cf[:, CF_RPB:CF_RPB + 128], AF.Exp, [cf, lg], [tB], scale=lg[:, 4 + h:5 + h])
                kb.tt(tB[:, :], tB[:, :], cf[:, CF_INDB:CF_INDB + 128], ALU.mult, [tB, cf], [tB])
                kb.tt(rtab[:, h, :], tA[:, :], tB[:, :], ALU.add, [tA, tB], [rtab])
                kb.act(qtab[:, h, :], cf[0:64, CF_QP1:CF_QP1 + 128], AF.Exp, [cf, lg], [qtab], scale=lg[0:64, h:h + 1])
                kb.act(qtab[:, 4 + h, :], cf[0:64, CF_CMQ:CF_CMQ + 128], AF.Exp, [cf, lg], [qtab], scale=lg[0:64, 4 + h:5 + h])
                kb.act(rcols[:, h:h + 1], cf[:, CF_COLS:CF_COLS + 1], AF.Exp, [cf, lg], [rcols], scale=lg[:, h:h + 1])
                kb.act(rcols[:, 4 + h:5 + h], cf[:, CF_COLS + 1:CF_COLS + 2], AF.Exp, [cf, lg], [rcols], scale=lg[:, 4 + h:5 + h])
            kb.act(rcols[:, 8:16], lg[:, 0:8], AF.Exp, [lg], [rcols], scale=128.0)

            if KSTOP <= 2:
                return
            kb.phase()
            Lr = {"rf": kb.ring(8, 512, F32, "rf"), "rb": kb.ring(4, 512, BF16, "rb")}
            cin = kb.carve(2 * 512, F32, "cin", 2)
            cbf16 = kb.carve(2 * 512, BF16, "cbf16", 2)
            krf = kb.carve(512, F32, "krf")
            krb = kb.carve(512, BF16, "krb")
            kouts = kb.ring(2, 512, BF16, "kout")
            vouts = kb.ring(2, 512, BF16, "vout")
            kb.load(cin[:, :, :], D["ckvc"][i], [], [cin])
            kb.load(krf[0:32, :], D["kropec"][i], [], [krf])
            kb.copy(cbf16[:, :, :], cin[:, :, :], [cin], [cbf16])
            kb.copy(krb[0:32, :], krf[0:32, :], [krf], [krb])

            def k_heads(ckvn_bf, kr_bf, rope, kcol0):
                for h in range(8):
                    ps = kb.bank()
                    kb.mm(ps[0:96, :], WK[:, 0, h * 96:(h + 1) * 96], ckvn_bf[:, 0, :], True, False, [WK, ckvn_bf], [ps])
                    kb.mm(ps[0:96, :], WK[:, 1, h * 96:(h + 1) * 96], ckvn_bf[:, 1, :], False, False, [WK, ckvn_bf], [ps])
                    kb.mm(ps[0:96, :], cbf[0:32, CB_WK3:CB_WK3 + 96], kr_bf[0:32, :], False, True, [cbf, kr_bf], [ps])
                    ko = kouts.next()
                    head_norm_rope(ps, 96, 96.0, ones_bf, pc[0:96, PE_KHG:PE_KHG + 1], rope, ko[0:96, :], ko, Lr)
                    kb.store(KT[h, 0:96, kcol0:kcol0 + 512], ko[0:96, :], [ko], [KT])

            def v_tok(ckvn_bf, krow0):
                for t in range(4):
                    ps = kb.bank()
                    kb.mm(ps[:, :], ckvn_bf[:, 0, t * 128:(t + 1) * 128], WV[:, 0, :], True, False, [ckvn_bf, WV], [ps])
                    kb.mm(ps[:, :], ckvn_bf[:, 1, t * 128:(t + 1) * 128], WV[:, 1, :], False, True, [ckvn_bf, WV], [ps])
                    vo = vouts.next()
                    kb.copy(vo[:, :], ps[:, :], [ps], [vo])
                    kb.store(VA[krow0 + t * 128:krow0 + (t + 1) * 128, :], vo[:, :], [vo], [VA])

            k_heads(cbf16, krb, None, 1024)
            v_tok(cbf16, 1024)

            if KSTOP <= 3:
                return
            kb.phase("P1")
            kb.banks = Ring(pbanks)
            R = make_rings()
            Lr = {"rf": R["t1"], "rb": R["sq"],
                  "xg": kb.carve(8 * 512, F32, "xg", 8), "hT": kb.carve(8 * 512, BF16, "hT", 8),
                  "rstd": kb.carve(512, F32, "rstd")}
            cq_sb = kb.carve(3 * 512, F32, "cq_sb", 3)
            cqn = kb.carve(3 * 512, BF16, "cqn", 3)
            ckv_sb = kb.carve(2 * 512, F32, "ckv_sb", 2)
            ckvn_f = kb.carve(2 * 512, F32, "ckvn_f", 2)
            ckvn_b = kb.carve(2 * 512, BF16, "ckvn_b", 2)
            kr_f = kb.carve(512, F32, "kr_f")
            kr_b = kb.carve(512, BF16, "kr_b")
            rsq = kb.carve(512, F32, "rsq")
            cosA = kb.carve(512, F32, "cosA")
            sinA = kb.carve(512, F32, "sinA")
            cosB = kb.carve(512, F32, "cosB")
            sinB = kb.carve(512, F32, "sinB")
            vouts = kb.ring(3, 512, BF16, "vout")
            for g in range(NG):
                j = 0 if g < 2 else 1
                samp = g >= 2
                t0 = g * 512
                xg, hT = x_norm_h(xsrc, g, j, Lr)
                ropeA = ropeB = None
                if samp:
                    s0 = (g - 2) * 512
                    kb.load(cosA[0:96, :], D["ropeA"][0, :, s0:s0 + 512], [], [cosA])
                    kb.load(sinA[0:96, :], D["ropeA"][1, :, s0:s0 + 512], [], [sinA])
                    kb.load(cosB[:, :], D["ropeB"][0, :, s0:s0 + 512], [], [cosB])
                    kb.load(sinB[:, :], D["ropeB"][1, :, s0:s0 + 512], [], [sinB])
                    ropeA = (cbf[:, CB_PERMA:CB_PERMA + 128], cosA, sinA)
                    ropeB = (cbf[:, CB_PERMB:CB_PERMB + 128], cosB, sinB)
                jobs = []

                def multi_job(nch, col0, raw_sb, Dn, gcol0, finish):
                    stt_ = {}

                    def A():
                        sqs = []
                        for c in range(nch):
                            ps = kb.bank()
                            inproj(hT, col0 + c * 128, 128, ps)
                            kb.copy(raw_sb[:, c, :], ps[:, :], [ps], [raw_sb], eng="act")
                            sq = R["sq"].next()
                            kb.act(sq[:, :], ps[:, :], AF.Square, [ps], [sq])
                            sqs.append(sq)
                        stt_["sqs"] = sqs

                    def B():
                        pss = kb.bank()
                        for c, sq in enumerate(stt_["sqs"]):
                            kb.mm(pss[:, :], ones_bf, sq[:, :], c == 0, c == nch - 1, [sq, cbf], [pss])
                        rstd_from(pss, 128, Dn, rsq, [])
                        finish()
                    return [A, B]

                def cq_fin():
                    for c in range(3):
                        kb.stt(cqn[:, c, :], cq_sb[:, c, :], pc[:, PE_QNG + c:PE_QNG + c + 1], rsq[:, :], ALU.mult, ALU.mult,
                               [cq_sb, pc, rsq], [cqn])

                def ckv_fin(samp=samp, t0=t0):
                    for c in range(2):
                        kb.stt(ckvn_f[:, c, :], ckv_sb[:, c, :], pc[:, PE_KVNG + c:PE_KVNG + c + 1], rsq[:, :], ALU.mult, ALU.mult,
                               [ckv_sb, pc, rsq], [ckvn_f])
                        kb.copy(ckvn_b[:, c, :], ckvn_f[:, c, :], [ckvn_f], [ckvn_b], eng="act")
                        if not samp:
                            o = kb.store(O[f"ckvT{i}"][c * 128:(c + 1) * 128, t0:t0 + 512], ckvn_f[:, c, :], [ckvn_f], [OUTT])
                            kb.finals.append(o)

                def krope_A(samp=samp, t0=t0):
                    ps = kb.bank()
                    inproj(hT, 640, 32, ps)
                    kb.copy(kr_f[0:32, :], ps[0:32, :], [ps], [kr_f], eng="act")
                    kb.copy(kr_b[0:32, :], kr_f[0:32, :], [kr_f], [kr_b])
                    if not samp:
                        o = kb.store(O[f"kropeT{i}"][:, t0:t0 + 512], kr_f[0:32, :], [kr_f], [OUTT])
                        kb.finals.append(o)

                def gate_A(c, t0=t0):
                    def A():
                        c0 = 672 + c * 128 if c < 4 else 2208 + (c - 4) * 128
                        ps = kb.bank()
                        inproj(hT, c0, 128, ps)
                        go = R["ko"].next()
                        kb.act(go[:, :], ps[:, :], AF.Silu, [ps], [go])
                        kb.store(GT[c, :, t0:t0 + 512], go[:, :], [go], [GT])
                    return A

                def qhead_job(h, t0=t0):
                    def mmf(ps):
                        for c in range(3):
                            kb.mm(ps[0:96, :], WUQ[:, c, h * 96:(h + 1) * 96], cqn[:, c, :], c == 0, c == 2, [WUQ, cqn], [ps])
                    return hnr_job(mmf, 96, 96.0, ones_bf, pc[0:96, PE_QHG:PE_QHG + 1], ropeA, R,
                                   post=lambda ko: kb.store(QT[h, 0:96, t0:t0 + 512], ko[0:96, :], [ko], [QT]))

                def khead_job(h, kc0):
                    def mmf(ps):
                        kb.mm(ps[0:96, :], WK[:, 0, h * 96:(h + 1) * 96], ckvn_b[:, 0, :], True, False, [WK, ckvn_b], [ps])
                        kb.mm(ps[0:96, :], WK[:, 1, h * 96:(h + 1) * 96], ckvn_b[:, 1, :], False, False, [WK, ckvn_b], [ps])
                        kb.mm(ps[0:96, :], cbf[0:32, CB_WK3:CB_WK3 + 96], kr_b[0:32, :], False, True, [cbf, kr_b], [ps])
                    return hnr_job(mmf, 96, 96.0, ones_bf, pc[0:96, PE_KHG:PE_KHG + 1], ropeA, R,
                                   post=lambda ko: kb.store(KT[h, 0:96, kc0:kc0 + 512], ko[0:96, :], [ko], [KT]))

                def vtok_A(t, kr0):
                    def A():
                        ps = kb.bank()
                        kb.mm(ps[:, :], ckvn_b[:, 0, t * 128:(t + 1) * 128], WV[:, 0, :], True, False, [ckvn_b, WV], [ps])
                        kb.mm(ps[:, :], ckvn_b[:, 1, t * 128:(t + 1) * 128], WV[:, 1, :], False, True, [ckvn_b, WV], [ps])
                        vo = vouts.next()
                        kb.copy(vo[:, :], ps[:, :], [ps], [vo])
                        kb.store(VA[kr0 + t * 128:kr0 + (t + 1) * 128, :], vo[:, :], [vo], [VA])
                    return A

                def rqk_job(c00, c, dst, scl, samp=samp, t0=t0):
                    stt_ = {}

                    def A():
                        ps = kb.bank()
                        inproj(hT, c00 + c * 128, 128, ps)
                        ko = R["ko"].next()
                        stt_["ko"] = ko
                        if not samp:
                            kb.act(ko[:, :], ps[:, :], AF.Identity, [ps], [ko], scale=scl)
                            kb.store(dst[c, :, t0:t0 + 512], ko[:, :], [ko], [dst])
                        else:
                            xf = R["xf"].next()
                            kb.act(xf[:, :], ps[:, :], AF.Identity, [ps], [xf], scale=scl)
                            xb = R["xb"].next()
                            kb.copy(xb[:, :], xf[:, :], [xf], [xb])
                            stt_["xf"], stt_["xb"] = xf, xb

                    def B():
                        if samp:
                            rope_tail_stage(stt_["xf"], stt_["xb"], 128, ropeB, stt_["ko"][:, :], stt_["ko"], R)
                            kb.store(dst[c, :, t0:t0 + 512], stt_["ko"][:, :], [stt_["ko"]], [dst])
                    return [A, B]

                def rv_A(t, t0=t0):
                    def A():
                        ps = kb.bank()
                        for c in range(8):
                            kb.mm(ps[:, :], hT[:, c, t * 128:(t + 1) * 128], WIN[:, c, 1696:2208], c == 0, c == 7, [hT, WIN], [ps])
                        vo = vouts.next()
                        kb.copy(vo[:, :], ps[:, :], [ps], [vo])
                        kb.store(RV[t0 + t * 128:t0 + (t + 1) * 128, :], vo[:, :], [vo], [RV])
                    return A

                heads_ = [qhead_job(h) for h in range(8)] + [khead_job(h, kbase(g)) for h in range(8)]
                fill_ = []
                for c00, dst, scl in ((1184, RQT, 0.125), (1440, RKT, 1.0)):
                    for c in range(2):
                        fill_.append(rqk_job(c00, c, dst, scl))
                fill_ += [[rv_A(t)] for t in range(4)]
                jobs.append(multi_job(3, 0, cq_sb, 384.0, PE_QNG, cq_fin))
                jobs.append(multi_job(2, 384, ckv_sb, 256.0, PE_KVNG, ckv_fin))
                jobs.append([krope_A])
                jobs.append(fill_[0])
                jobs.append(fill_[1])
                fi = 2
                for k_ in range(16):
                    jobs.append(heads_[k_])
                    if k_ % 2 == 1 and fi < len(fill_):
                        jobs.append(fill_[fi])
                        fi += 1
                    if k_ >= 10 and k_ % 2 == 0:
                        jobs.append([vtok_A((k_ - 10) // 2, kbase(g))]) if (k_ - 10) // 2 < 4 else None
                jobs.append([vtok_A(3, kbase(g))])
                for c in range(8):
                    jobs.append([gate_A(c)])
                run_pipeline(jobs)
            kb.hooks.setdefault("P2", []).append(lambda l=l: prefetch_in(l + 1))

            if KSTOP <= 4:
                return
            kb.banks = Ring(pbanks[0:6])
            kb.phase("P2")
            Lr = {"rf": kb.ring(12, 512, F32, "rf"), "rrow": kb.ring(6, 512, BF16, "rrow")}
            ppairs = kb.ring(4, 1024, BF16, "pT")
            kts = kb.ring(2, 4608, BF16, "kts")
            qts = kb.ring(2, 4096, BF16, "qts")
            vas = kb.ring(2, 36 * 65, BF16, "vas", 36)
            ktp = kb.ring(2, 1024, BF16, "ktp")
            qtp = kb.ring(2, 1024, BF16, "qtp")
            vap = kb.ring(2, 8 * 65, BF16, "vap", 8)
            gts = kb.ring(4, 512, BF16, "gts")
            mos = kb.ring(4, 512, BF16, "mos")
            for va in vas.items + vap.items:
                kb.memset(va[:, :, 64:65], 1.0, [va], eng="dve")
            scale = 96.0 ** -0.5
            AS = AStream()
            PF = 3
            for h in range(8):
                hb0 = len(AS.blocks)
                ktP, qtP, vaP = ktp.next(), qtp.next(), vap.next()
                kt, qt, va = kts.next(), qts.next(), vas.next()

                def head_loads(h=h, ktP=ktP, qtP=qtP, vaP=vaP, kt=kt, qt=qt, va=va):
                    kb.load(ktP[0:96, 0:1024], KT[h, 0:96, 0:1024], [KT], [ktP])
                    kb.load(qtP[0:96, 0:1024], QT[h, 0:96, 0:1024], [QT], [qtP])
                    kb.load(vaP[:, 0:8, 0:64], VA.ap[0:1024, h * 64:(h + 1) * 64].rearrange("(t p) d -> p t d", p=128), [VA], [vaP])
                    kb.load(kt[0:96, 0:4608], KT[h, 0:96, 1024:5632], [KT], [kt])
                    kb.load(qt[0:96, 0:4096], QT[h, 0:96, 1024:5120], [QT], [qt])
                    for t6 in range(6):
                        kb.load(va[:, t6 * 6:(t6 + 1) * 6, 0:64],
                                VA.ap[1024 + t6 * 768:1024 + (t6 + 1) * 768, h * 64:(h + 1) * 64].rearrange("(t p) d -> p t d", p=128), [VA], [va])
                head_pre = head_loads

                def mk_post(h, tk0):
                    gt, mo = gts.next(), mos.next()

                    def gload():
                        kb.load(gt[0:64, :], GT[h // 2, (h % 2) * 64:(h % 2) * 64 + 64, tk0:tk0 + 512], [GT], [gt])

                    def post(rr, osb):
                        attn_finish_b(rr, osb, gt[0:64, :], gt, mo[0:64, :], mo, Lr,
                                      after=lambda: kb.store(MT[h // 2, (h % 2) * 64:(h % 2) * 64 + 64, tk0:tk0 + 512], mo[0:64, :], [mo], [MT]))
                    return gload, post

                for pr in range(2):
                    acc = accb.next()
                    segs = []
                    for s2 in range(2):
                        sq_ = pr * 2 + s2
                        ktl = [(ktP[0:96, sq_ * 256 + u * 128:sq_ * 256 + (u + 1) * 128], vaP[:, sq_ * 2 + u, :], None) for u in range(2)]
                        segs.append((s2 * 256, 256, ktl, [ktP, qtP, vaP], 96, scale))
                    gload, post = mk_post(h, pr * 512)
                    bi = AS.add(lambda c0, n, qtP=qtP, pr=pr: qtP[0:96, pr * 512 + c0:pr * 512 + c0 + n], segs, acc, None, post)
                    AS.attach(bi, gload)
                for qg in range(8):
                    acc = accb.next()
                    ktl = [(kt[0:96, u * 128:(u + 1) * 128], va[:, u, :], None) for u in range(36)]
                    segs = [(0, 512, ktl, [kt, qt, va], 96, scale)]
                    gload, post = mk_post(h, 1024 + qg * 512)
                    bi = AS.add(lambda c0, n, qt=qt, qg=qg: qt[0:96, qg * 512 + c0:qg * 512 + c0 + n], segs, acc, None, post)
                    AS.attach(bi, gload)
                AS.attach(hb0 - PF, head_pre)
            AS.run(ppairs, Lr)

            if KSTOP <= 5:
                return
            kb.phase()
            Lr = {"rf": kb.ring(8, 512, F32, "rf"), "rb": kb.ring(4, 512, BF16, "rb")}
            rqs = kb.ring(2, 4096, BF16, "rqs")
            rks = kb.ring(2, 4096, BF16, "rks")
            rvs = kb.ring(2, 32 * 128, BF16, "rvs", 32)
            kwf = kb.carve(32 * 64, BF16, "kwf", 32)
            kwb = kb.carve(32 * 64, BF16, "kwb", 32)
            Rfb = kb.carve(33 * 128, BF16, "Rfb", 33)
            Rbb = kb.carve(33 * 128, BF16, "Rbb", 33)
            Rst = kb.ring(6, 128, F32, "Rst")
            smr = kb.ring(2, 512, BF16, "smr")
            qfr = kb.ring(2, 512, BF16, "qfr")
            qbr = kb.ring(2, 512, BF16, "qbr")
            gts = kb.ring(2, 512, BF16, "gts")
            mos = kb.ring(2, 512, BF16, "mos")
            ident_bf = cbf[:, CB_IDENT:CB_IDENT + 128]
            for h in range(4):
                ch, r0 = h // 2, (h % 2) * 64
                for sq_i in range(5):
                    samp = sq_i == 4
                    tk0 = 1024 if samp else sq_i * 256
                    nch = 32 if samp else 2
                    L = nch * 128
                    rq, rk, rv = rqs.next(), rks.next(), rvs.next()
                    kb.load(rq[0:64, 0:L], RQT[ch, r0:r0 + 64, tk0:tk0 + L], [RQT], [rq])
                    kb.load(rk[0:64, 0:L], RKT[ch, r0:r0 + 64, tk0:tk0 + L], [RKT], [rk])
                    for t8 in range(0, nch, 8):
                        te = min(nch, t8 + 8)
                        kb.load(rv[:, t8:te, :], RV.ap[tk0 + t8 * 128:tk0 + te * 128, h * 128:(h + 1) * 128].rearrange("(t p) d -> p t d", p=128), [RV], [rv])
                    for n0 in range(0, nch, 4):
                        nn = min(4, nch - n0)
                        ps = kb.bank()
                        for u in range(nn):
                            n = n0 + u
                            kb.mm(ps[:, u * 64:(u + 1) * 64], rk[0:64, n * 128:(n + 1) * 128], ident_bf[0:64, 0:64], True, True, [rk, cbf], [ps])
                        psv = ps[:, 0:nn * 64].rearrange("p (c d) -> p c d", c=nn)
                        kb.ts(kwf[:, n0:n0 + nn, :], psv, rcols[:, h:h + 1], None, ALU.mult, None, [ps, rcols], [kwf])
                        kb.act(kwb[:, n0:n0 + nn, :], psv, AF.Identity, [ps, rcols], [kwb], scale=rcols[:, 4 + h:5 + h])
                    if h == 0 and sq_i == 0:
                        kb.dbg("kwf", kwf[:, 0:2, :], kwf, BF16)
                        kb.dbg("kwb", kwb[:, 0:2, :], kwb, BF16)
                        kb.dbg("rk", rk[0:64, 0:256], rk, BF16)
                        kb.dbg("rq", rq[0:64, 0:256], rq, BF16)
                        kb.dbg("rv", rv[:, 0:2, :], rv, BF16)
                        kb.dbg("rcols", rcols[:, 0:16], rcols)
                        kb.dbg("rtab", rtab[:, 0, :], rtab)
                        kb.dbg("qtab", qtab[:, :, :], qtab)
                    if KSUB <= 1:
                        continue
                    Rcf = Rst.next()
                    Rcb = Rst.next()
                    if samp:
                        kb.load(Rcf[0:64, :], D["st_f"][i, :, h, :], [], [Rcf])
                        kb.load(Rcb[0:64, :], D["st_b"][i, :, h, :], [], [Rcb])
                    else:
                        kb.memset(Rcf[0:64, :], 0.0, [Rcf], eng="dve")
                        kb.memset(Rcb[0:64, :], 0.0, [Rcb], eng="dve")
                    for n in range(nch):
                        nb_ = nch - 1 - n
                        kb.copy(Rfb[0:64, n, :], Rcf[0:64, :], [Rcf], [Rfb], eng="act")
                        ps = kb.bank()
                        kb.mm(ps[0:64, 0:128], kwf[:, n, :], rv[:, n, :], True, True, [kwf, rv], [ps])
                        Rn = Rst.next()
                        kb.stt(Rn[0:64, :], Rcf[0:64, :], rcols[0:64, 8 + h:9 + h], ps[0:64, 0:128], ALU.mult, ALU.add,
                               [Rcf, rcols, ps], [Rn])
                        Rcf = Rn
                        kb.copy(Rbb[0:64, nb_, :], Rcb[0:64, :], [Rcb], [Rbb], eng="act")
                        ps = kb.bank()
                        kb.mm(ps[0:64, 0:128], kwb[:, nb_, :], rv[:, nb_, :], True, True, [kwb, rv], [ps])
                        Rn = Rst.next()
                        kb.stt(Rn[0:64, :], Rcb[0:64, :], rcols[0:64, 12 + h:13 + h], ps[0:64, 0:128], ALU.mult, ALU.add,
                               [Rcb, rcols, ps], [Rn])
                        Rcb = Rn
                    if not samp:
                        o = kb.store(O[f"retf{i}"][sq_i, h, :, :], Rcf[0:64, :], [Rcf], [OUTT])
                        kb.finals.append(o)
                        o = kb.store(O[f"retb{i}"][sq_i, h, :, :], Rcb[0:64, :], [Rcb], [OUTT])
                        kb.finals.append(o)
                    if h == 0 and sq_i == 0:
                        kb.dbg("Rfb", Rfb[0:64, 0:2, :], Rfb, BF16)
                        kb.dbg("Rbb", Rbb[0:64, 0:2, :], Rbb, BF16)
                    if KSUB <= 3:
                        continue
                    nblk = (nch + 3) // 4
                    for blk in range(nblk):
                        cpb = min(4, nch - blk * 4)
                        W = cpb * 128
                        acc = accb.next()
                        nb0 = blk * 4
                        ps = kb.bank()
                        for u in range(cpb):
                            n = nb0 + u
                            kb.mm(ps[:, u * 128:(u + 1) * 128], rk[0:64, n * 128:(n + 1) * 128], rq[0:64, n * 128:(n + 1) * 128], True, True, [rk, rq], [ps])
                        sm = smr.next()
                        kb.tt(sm[:, 0:W].rearrange("p (c q) -> p c q", c=cpb), ps[:, 0:W].rearrange("p (c q) -> p c q", c=cpb),
                              rtab[:, h:h + 1, :].broadcast_to([128, cpb, 128]), ALU.mult, [ps, rtab], [sm])
                        qf, qb_ = qfr.next(), qbr.next()
                        rqv = rq[0:64, nb0 * 128:nb0 * 128 + W].rearrange("p (c q) -> p c q", c=cpb)
                        kb.tt(qf[0:64, 0:W].rearrange("p (c q) -> p c q", c=cpb), rqv,
                              qtab[:, h:h + 1, :].broadcast_to([64, cpb, 128]), ALU.mult, [rq, qtab], [qf])
                        kb.tt(qb_[0:64, 0:W].rearrange("p (c q) -> p c q", c=cpb), rqv,
                              qtab[:, 4 + h:5 + h, :].broadcast_to([64, cpb, 128]), ALU.mult, [rq, qtab], [qb_])
                        for u in range(cpb):
                            n = nb0 + u
                            cs = slice(u * 128, (u + 1) * 128)
                            kb.mm(acc[:, cs], rv[:, n, :], sm[:, cs], True, False, [rv, sm], [acc])
                            kb.mm(acc[:, cs], Rfb[0:64, n, :], qf[0:64, cs], False, False, [Rfb, qf], [acc])
                            kb.mm(acc[:, cs], Rbb[0:64, n, :], qb_[0:64, cs], False, True, [Rbb, qb_], [acc])
                        xs = Lr["rf"].next()
                        kb.copy(xs[:, 0:W], acc[:, 0:W], [acc], [xs], eng="act")
                        xb = Lr["rb"].next()
                        kb.copy(xb[:, 0:W], acc[:, 0:W], [acc], [xb])
                        sqb = Lr["rb"].next()
                        kb.act(sqb[:, 0:W], acc[:, 0:W], AF.Square, [acc], [sqb])
                        pm, pq = kb.bank(), kb.bank()
                        kb.mm(pm[:, 0:W], ones_bf, xb[:, 0:W], True, True, [xb, cbf], [pm])
                        kb.mm(pq[:, 0:W], ones_bf, sqb[:, 0:W], True, True, [sqb, cbf], [pq])
                        mean = Lr["rf"].next()
                        kb.act(mean[:, 0:W], pm[:, 0:W], AF.Identity, [pm], [mean], scale=1.0 / 128)
                        m2 = Lr["rf"].next()
                        kb.tt(m2[:, 0:W], mean[:, 0:W], mean[:, 0:W], ALU.mult, [mean], [m2])
                        var = Lr["rf"].next()
                        kb.stt(var[:, 0:W], pq[:, 0:W], 1.0 / 128, m2[:, 0:W], ALU.mult, ALU.subtract, [pq, m2], [var])
                        rs = Lr["rf"].next()
                        kb.act(rs[:, 0:W], var[:, 0:W], AF.Ln, [var, cf], [rs], bias=eps_c, scale=1.0)
                        kb.act(rs[:, 0:W], rs[:, 0:W], AF.Exp, [rs], [rs], scale=-0.5)
                        xc = Lr["rf"].next()
                        kb.tt(xc[:, 0:W], xs[:, 0:W], mean[:, 0:W], ALU.subtract, [xs, mean], [xc])
                        y = Lr["rf"].next()
                        kb.stt(y[:, 0:W], xc[:, 0:W], pc[:, PE_RNG + h:PE_RNG + h + 1], rs[:, 0:W], ALU.mult, ALU.mult, [xc, pc, rs], [y])
                        gt, mo = gts.next(), mos.next()
                        tb0 = tk0 + blk * 512
                        kb.load(gt[:, 0:W], GT[4 + h, :, tb0:tb0 + W], [GT], [gt])
                        kb.tt(mo[:, 0:W], y[:, 0:W], gt[:, 0:W], ALU.mult, [y, gt], [mo])
                        kb.store(MT[4 + h, :, tb0:tb0 + W], mo[:, 0:W], [mo], [MT])

            if KSTOP <= 6:
                return
            phase_out(l, xsrc, xdst_ap, xdst_t)
            kb.hooks.setdefault("P1", []).append(lambda l=l: prefetch_out(l + 1))

        def odd_layer(l, xsrc, xdst_ap, xdst_t):
            i = l // 2
            set_layer(l)
            WINo = T(flat(WIN)[:, 0:8 * 2560].rearrange("p (a b) -> p a b", a=8), "WINo")
            WINo.b = WIN.b
            sinkE = kb.carve(2048, F32, "sinkE")
            kb.load(sinkE[0:1, :], D["sinkrow"][i], [], [sinkE])
            kb.act(sinkE[0:1, :], sinkE[0:1, :], AF.Exp, [sinkE, cf], [sinkE], bias=nshift_c[0:1, :], scale=1.0)
            kb.copy(sinkB[0:1, :], sinkE[0:1, :], [sinkE], [sinkB])

            def inproj_o(hT, c0, m, ps):
                for c in range(8):
                    kb.mm(ps[0:m, :], WINo[:, c, c0:c0 + m], hT[:, c, :], c == 0, c == 7, [WINo, hT], [ps])

            kb.phase()
            cin = kb.carve(2 * 512, F32, "cin", 2)
            cb2 = kb.carve(2 * 512, BF16, "cb2", 2)
            vin = kb.carve(4 * 256, F32, "vin", 4)
            vb2 = kb.carve(4 * 256, BF16, "vb2", 4)
            kb.load(cin[:, :, :], D["wkc"][i], [], [cin])
            kb.copy(cb2[:, :, :], cin[:, :, :], [cin], [cb2])
            for c in range(2):
                kb.store(KT[c, :, 1024:1536], cb2[:, c, :], [cb2], [KT])
            kb.load(vin[:, :, :], D["wvc"][i], [], [vin])
            kb.copy(vb2[:, :, :], vin[:, :, :], [vin], [vb2])
            kb.store(VA.ap[1024:1536, 0:256].rearrange("(t p) d -> p t d", p=128), vb2[:, :, :], [vb2], [VA])

            kb.phase("P1")
            kb.banks = Ring(pbanks)
            R = make_rings()
            Lr = {"rf": R["t1"], "rb": R["sq"],
                  "xg": kb.carve(8 * 512, F32, "xg", 8), "hT": kb.carve(8 * 512, BF16, "hT", 8),
                  "rstd": kb.carve(512, F32, "rstd")}
            cosB = kb.carve(512, F32, "cosB")
            sinB = kb.carve(512, F32, "sinB")
            kf32 = kb.ring(3, 512, F32, "kf32")
            vouts = kb.ring(3, 256, BF16, "vout")
            vf32 = kb.ring(3, 256, F32, "vf32")
            for g in range(NG):
                j = 0 if g < 2 else 1
                samp = g >= 2
                t0 = g * 512
                xg, hT = x_norm_h(xsrc, g, j, Lr)
                ropeB = None
                if samp:
                    s0 = (g - 2) * 512
                    kb.load(cosB[:, :], D["ropeB"][0, :, s0:s0 + 512], [], [cosB])
                    kb.load(sinB[:, :], D["ropeB"][1, :, s0:s0 + 512], [], [sinB])
                    ropeB = (cbf[:, CB_PERMB:CB_PERMB + 128], cosB, sinB)
                jobs = []

                def q_job(c, t0=t0):
                    return hnr_job(lambda ps: inproj_o(hT, c * 128, 128, ps), 128, 64.0, ones64_bf, pc[:, PO_Q64:PO_Q64 + 1], ropeB, R,
                                   post=lambda ko: kb.store(QT[c, :, t0:t0 + 512], ko[:, :], [ko], [QT]))

                def k_job(c, samp=samp, t0=t0, kc0=kbase(g)):
                    hold = {}

                    def f32fn():
                        hold["kf"] = kf32.next()
                        return hold["kf"]

                    def post(ko):
                        if not samp:
                            kf = hold["kf"]
                            o = kb.store(O[f"wkT{i}"][c * 128:(c + 1) * 128, t0:t0 + 512], kf[:, :], [kf], [OUTT])
                            kb.finals.append(o)
                        kb.store(KT[c, :, kc0:kc0 + 512], ko[:, :], [ko], [KT])
                    return hnr_job(lambda ps: inproj_o(hT, 1024 + c * 128, 128, ps), 128, 64.0, ones64_bf, pc[:, PO_K64:PO_K64 + 1],
                                   ropeB, R, post=post, f32_out_fn=(None if samp else f32fn))

                def v_A(t, samp=samp, t0=t0, kc0=kbase(g)):
                    def A():
                        ps = kb.bank()
                        for c in range(8):
                            kb.mm(ps[:, 0:256], hT[:, c, t * 128:(t + 1) * 128], WINo[:, c, 1280:1536], c == 0, c == 7, [hT, WINo], [ps])
                        vo = vouts.next()
                        kb.copy(vo[:, :], ps[:, 0:256], [ps], [vo])
                        kb.store(VA.ap[kc0 + t * 128:kc0 + (t + 1) * 128, 0:256], vo[:, :], [vo], [VA])
                        if not samp:
                            vf = vf32.next()
                            kb.copy(vf[:, :], ps[:, 0:256], [ps], [vf])
                            o = kb.store(O[f"wvo{i}"][t0 + t * 128:t0 + (t + 1) * 128, :], vf[:, :], [vf], [OUTT])
                            kb.finals.append(o)
                    return A

                def g_A(c, t0=t0):
                    def A():
                        ps = kb.bank()
                        inproj_o(hT, 1536 + c * 128, 128, ps)
                        go = R["ko"].next()
                        kb.act(go[:, :], ps[:, :], AF.Silu, [ps], [go])
                        kb.store(GT[c, :, t0:t0 + 512], go[:, :], [go], [GT])
                    return A

                for c in range(8):
                    jobs.append(q_job(c))
                    if c % 2 == 1:
                        jobs.append([v_A(c // 2)])
                for c in range(2):
                    jobs.append(k_job(c))
                for c in range(8):
                    jobs.append([g_A(c)])
                run_pipeline(jobs)
            kb.hooks.setdefault("P2", []).append(lambda l=l: prefetch_in(l + 1))

            kb.banks = Ring(pbanks[0:6])
            kb.phase("P2")
            Lr = {"rf": kb.ring(9, 512, F32, "rf"), "rrow": kb.ring(6, 512, BF16, "rrow"), "act_recip": True}
            ppairs = kb.ring(4, 1024, BF16, "pT")
            kts = kb.ring(2, 4608, BF16, "kts")
            vas = kb.ring(2, 36 * 65, BF16, "vas", 36)
            ktp = kb.ring(2, 1024, BF16, "ktp")
            vap = kb.ring(2, 8 * 65, BF16, "vap", 8)
            qgs = kb.ring(3, 4 * 512, BF16, "qgs", 4)
            gts = kb.ring(4, 4 * 512, BF16, "gts", 4)
            mos = kb.ring(2, 4 * 512, BF16, "mos", 4)
            for va in vas.items + vap.items:
                kb.memset(va[:, :, 64:65], 1.0, [va], eng="dve")
            for tz in kts.items + ktp.items:
                kb.memset(tz[64:128, :], 0.0, [tz], eng="dve")
            for tz in qgs.items:
                kb.memset(tz[64:128, :, :], 0.0, [tz], eng="dve")
            scale = 0.125
            mprev = cbf[:, CB_MPREV:CB_MPREV + 512]
            mnext = cbf[:, CB_MNEXT:CB_MNEXT + 512]
            e64 = cf[0:1, CF_E64:CF_E64 + 65]
            QTh = QT.ap.rearrange("c (two r) t -> (c two) r t", two=2)
            GTh = GT.ap.rearrange("c (two r) t -> (c two) r t", two=2)
            MTh = MT.ap.rearrange("c (two r) t -> (c two) r t", two=2)
            KTh = KT.ap.rearrange("c (two r) t -> (c two) r t", two=2)
            AS = AStream()
            PF = 4
            for kvh in range(4):
                sink_ap = sinkB[0:1, kvh * 512:(kvh + 1) * 512]
                extra = (e64b[0:1, 0:65], sink_ap, [e64b, sinkB])
                hb0 = len(AS.blocks)
                ktP, vaP = ktp.next(), vap.next()
                kt, va = kts.next(), vas.next()

                def head_loads(kvh=kvh, ktP=ktP, vaP=vaP, kt=kt, va=va):
                    kb.load(ktP[0:64, 0:1024], KTh[kvh, :, 0:1024], [KT], [ktP])
                    kb.load(vaP[:, 0:8, 0:64], VA.ap[0:1024, kvh * 64:(kvh + 1) * 64].rearrange("(t p) d -> p t d", p=128), [VA], [vaP])
                    kb.load(kt[0:64, 0:4608], KTh[kvh, :, 1024:5632], [KT], [kt])
                    for t6 in range(6):
                        kb.load(va[:, t6 * 6:(t6 + 1) * 6, 0:64],
                                VA.ap[1024 + t6 * 768:1024 + (t6 + 1) * 768, kvh * 64:(kvh + 1) * 64].rearrange("(t p) d -> p t d", p=128), [VA], [va])

                for ck in range(10):
                    tk0 = ck * 512
                    qg, gt, mo = qgs.next(), gts.next(), mos.next()

                    def chunk_loads(kvh=kvh, qg=qg, gt=gt, tk0=tk0):
                        kb.load(qg[0:64, :, :], QTh[kvh * 4:(kvh + 1) * 4, :, tk0:tk0 + 512].rearrange("h r t -> r h t"), [QT], [qg])
                        kb.load(gt[0:64, :, :], GTh[kvh * 4:(kvh + 1) * 4, :, tk0:tk0 + 512].rearrange("h r t -> r h t"), [GT], [gt])
                    cb0 = len(AS.blocks)
                    for qb in range(4):
                        acc = accb.next()
                        if ck < 2:
                            sq_ = ck * 2 + qb // 2
                            ktl = [(ktP[0:128, sq_ * 256 + u * 128:sq_ * 256 + (u + 1) * 128], vaP[:, sq_ * 2 + u, :], None) for u in range(2)]
                            rd = [ktP, qg, vaP]
                        else:
                            bi_ = (ck - 2) * 4 + qb
                            ktl = [(kt[0:128, u * 128:(u + 1) * 128], va[:, u, :], None) for u in range(4)]
                            if bi_ > 0:
                                u = 4 + bi_ - 1
                                ktl.append((kt[0:128, u * 128:(u + 1) * 128], va[:, u, :], mprev))
                            u = 4 + bi_
                            ktl.append((kt[0:128, u * 128:(u + 1) * 128], va[:, u, :], None))
                            if bi_ < 31:
                                u = 4 + bi_ + 1
                                ktl.append((kt[0:128, u * 128:(u + 1) * 128], va[:, u, :], mnext))
                            rd = [kt, qg, va]
                        segs = [(0, 512, ktl, rd, 64, scale)]
                        aft = None
                        if qb == 3:
                            aft = (lambda mo=mo, tk0=tk0, kvh=kvh: kb.store(
                                MTh[kvh * 4:(kvh + 1) * 4, :, tk0:tk0 + 512].rearrange("h r t -> r h t"), mo[0:64, :, :], [mo], [MT]))

                        def post(rr, osb, gt=gt, mo=mo, qb=qb, aft=aft):
                            attn_finish_b(rr, osb, gt[0:64, :, qb * 128:(qb + 1) * 128], gt, mo[0:64, :, qb * 128:(qb + 1) * 128], mo, Lr, after=aft)
                        AS.add(lambda c0, n, qg=qg, qb=qb: qg[0:128, :, qb * 128:(qb + 1) * 128], segs, acc, extra, post)
                    AS.attach(cb0 - PF, chunk_loads)
                AS.attach(hb0 - PF, head_loads)
            AS.run(ppairs, Lr)

            phase_out(l, xsrc, xdst_ap, xdst_t)
            kb.hooks.setdefault("P1", []).append(lambda l=l: prefetch_out(l + 1))

        kb.load(pc.t[:, :, :], D["pc_all"][:, :, :], [], [pc])
        prefetch_in(0)
        prefetch_out(0)
        for l in range(NLAYERS):
            set_layer(l)
            if l % 2 == 0:
                phase_mod(l, pc, PE_NORMG, PE_ADAB)
            else:
                phase_mod(l, pc, PO_NORMG, PO_ADAB)
        for l in range(NLAYERS):
            xsrc = XIN if l == 0 else XT
            if l == NLAYERS - 1:
                xdst_ap, xdst_t = O["yT"], OUTT
            else:
                xdst_ap, xdst_t = XT.ap, XT
            if l % 2 == 0:
                even_layer(l, xsrc, xdst_ap, xdst_t)
            else:
                odd_layer(l, xsrc, xdst_ap, xdst_t)
        kb.S.emit(nc, kb.S.store_finals("st"))
    return nc


def _fm(v, ncol):
    return np.ascontiguousarray(v.reshape(ncol, 128).T)


def _wl(w):
    K, N = w.shape
    return np.ascontiguousarray(w.reshape(K // 128, 128, N).transpose(1, 0, 2))


def _prep_shared(inp):
    global _CONSTS
    if _CONSTS is None:
        _CONSTS = _build_consts()
    f = np.float32
    sh = dict(_CONSTS)
    sh["ada_w"] = np.stack([_wl(np.asarray(inp["ada_w"][l], f)) for l in range(4)])
    sh["w_in_e"] = np.stack([_wl(np.asarray(inp["ab_w_in"][i], f)) for i in range(2)])
    sh["w_in_o"] = np.stack([_wl(np.asarray(inp["win_w_in"][i], f)) for i in range(2)])
    wo = [inp["ab_w_out"][0], inp["win_w_out"][0], inp["ab_w_out"][1], inp["win_w_out"][1]]
    sh["w_out"] = np.stack([_wl(np.asarray(w, f)) for w in wo])
    sh["w_uq"] = np.stack([_wl(np.asarray(inp["mla_w_uq"][i], f)) for i in range(2)])
    wk = np.zeros((2, 256, 8, 96), f)
    wv = np.zeros((2, 256, 8, 64), f)
    for i in range(2):
        w = np.asarray(inp["mla_w_ukv"][i], f).reshape(256, 8, 128)
        wk[i, :, :, 0:64] = w[:, :, 0:64]
        wv[i] = w[:, :, 64:128]
    sh["wk"] = np.stack([_wl(wk[i].reshape(256, 768)) for i in range(2)])
    sh["wv"] = np.stack([_wl(wv[i].reshape(256, 512)) for i in range(2)])
    pce = np.zeros((2, 128, PE_N), f)
    pco = np.zeros((2, 128, PO_N), f)
    for i in range(2):
        pce[i, :, PE_QNG:PE_QNG + 3] = _fm(np.asarray(inp["mla_q_norm_g"][i], f), 3)
        pce[i, :, PE_KVNG:PE_KVNG + 2] = _fm(np.asarray(inp["mla_kv_norm_g"][i], f), 2)
        pce[i, 0:96, PE_QHG] = np.asarray(inp["mla_q_head_g"][i], f)
        pce[i, 0:96, PE_KHG] = np.asarray(inp["mla_k_head_g"][i], f)
        pce[i, :, PE_RNG:PE_RNG + 4] = _fm(np.asarray(inp["ret_norm_g"][i], f), 4)
        pce[i, :, PE_NORMG:PE_NORMG + 8] = _fm(np.asarray(inp["norm_g"][2 * i], f), 8)
        pce[i, :, PE_ADAB:PE_ADAB + 24] = _fm(np.asarray(inp["ada_b"][2 * i], f), 24)
        pco[i, :, PO_Q64] = np.tile(np.asarray(inp["win_q_head_g"][i], f), 2)
        pco[i, :, PO_K64] = np.tile(np.asarray(inp["win_k_head_g"][i], f), 2)
        pco[i, :, PO_NORMG:PO_NORMG + 8] = _fm(np.asarray(inp["norm_g"][2 * i + 1], f), 8)
        pco[i, :, PO_ADAB:PO_ADAB + 24] = _fm(np.asarray(inp["ada_b"][2 * i + 1], f), 24)
    pca = np.zeros((128, 4, 48), f)
    for i in range(2):
        pca[:, 2 * i, 0:PE_N] = pce[i]
        pca[:, 2 * i + 1, 0:PO_N] = pco[i]
    sh["pc_all"] = pca
    sh["sinkrow"] = np.ascontiguousarray(np.repeat(np.asarray(inp["win_sink"], f), 128, axis=1).reshape(2, 1, 2048))
    sh["decay"] = np.ascontiguousarray(np.asarray(inp["ret_decay"], f).reshape(2, 1, 8))
    return sh


def _prep_core(inp, c):
    f = np.float32
    b = c // 2
    m = {}
    xp = np.asarray(inp["x_prompt"][4 * c:4 * c + 4], f).reshape(1024, 1024)
    xs = np.asarray(inp["x_sample"][b], f)
    m["xin"] = np.ascontiguousarray(np.concatenate([xp, xs], 0).T)
    cond = np.stack([np.asarray(inp["c_ctx"], f), np.asarray(inp["c"][b], f)], -1)
    m["cond"] = np.ascontiguousarray(cond.reshape(8, 128, 2).transpose(1, 0, 2).reshape(128, 16))
    ck = [inp["cache_l0_mla_ckv"], inp["cache_l2_mla_ckv"]]
    kr = [inp["cache_l0_mla_krope"], inp["cache_l2_mla_krope"]]
    m["ckvc"] = np.stack([_wl(np.ascontiguousarray(np.asarray(x[b], f).T)) for x in ck])
    m["kropec"] = np.stack([np.ascontiguousarray(np.asarray(x[b], f).T) for x in kr])
    sf = [inp["state_l0_ret_fwd"], inp["state_l2_ret_fwd"]]
    sb_ = [inp["state_l0_ret_bwd"], inp["state_l2_ret_bwd"]]
    m["st_f"] = np.stack([np.ascontiguousarray(np.asarray(x[b], f).transpose(1, 0, 2)) for x in sf])
    m["st_b"] = np.stack([np.ascontiguousarray(np.asarray(x[b], f).transpose(1, 0, 2)) for x in sb_])
    wk_ = [inp["cache_l1_win_k"], inp["cache_l3_win_k"]]
    wv_ = [inp["cache_l1_win_v"], inp["cache_l3_win_v"]]
    m["wkc"] = np.stack([_wl(np.ascontiguousarray(np.asarray(x[b], f).reshape(512, 256).T)) for x in wk_])
    m["wvc"] = np.stack([np.ascontiguousarray(np.asarray(x[b], f).reshape(4, 128, 256).transpose(1, 0, 2)) for x in wv_])
    return m


_NC = None


def kernel(**inputs):
    global _NC
    if _NC is None:
        _NC = build()
    sh = _prep_shared(inputs)
    in_maps = []
    for c in range(8):
        m = dict(sh)
        m.update(_prep_core(inputs, c))
        in_maps.append(m)
    res = run_bass_kernel_spmd(_NC, in_maps, core_ids=list(range(8)))
    R = res.results
    f = np.float32
    y_p = np.zeros((32, 256, 1024), f)
    y_s = np.zeros((4, 4096, 1024), f)
    outs = {k: [] for k in ("ckv", "krope", "rf", "rb", "wk", "wv")}
    ckv = [np.zeros((32, 256, 256), f) for _ in range(2)]
    krope = [np.zeros((32, 256, 32), f) for _ in range(2)]
    rf = [np.zeros((32, 4, 64, 128), f) for _ in range(2)]
    rb = [np.zeros((32, 4, 64, 128), f) for _ in range(2)]
    wk = [np.zeros((32, 256, 4, 64), f) for _ in range(2)]
    wv = [np.zeros((32, 256, 4, 64), f) for _ in range(2)]
    for c in range(8):
        r = R[c]
        yT = np.asarray(r["yT"])
        y_p[4 * c:4 * c + 4] = yT[:, 0:1024].T.reshape(4, 256, 1024)
        b, half = c // 2, c % 2
        y_s[b, half * 2048:(half + 1) * 2048] = yT[:, 1024 + half * 2048:1024 + (half + 1) * 2048].T
        for i in range(2):
            ckv[i][4 * c:4 * c + 4] = np.asarray(r[f"ckvT{i}"]).T.reshape(4, 256, 256)
            krope[i][4 * c:4 * c + 4] = np.asarray(r[f"kropeT{i}"]).T.reshape(4, 256, 32)
            rf[i][4 * c:4 * c + 4] = np.asarray(r[f"retf{i}"])
            rb[i][4 * c:4 * c + 4] = np.asarray(r[f"retb{i}"])
            wk[i][4 * c:4 * c + 4] = np.asarray(r[f"wkT{i}"]).T.reshape(4, 256, 4, 64)
            wv[i][4 * c:4 * c + 4] = np.asarray(r[f"wvo{i}"]).reshape(4, 256, 4, 64)
    return (y_p, y_s, ckv[0], krope[0], rf[0], rb[0], wk[0], wv[0],
            ckv[1], krope[1], rf[1], rb[1], wk[1], wv[1])
```
